# Optimizing a Trainium2 kernel written in Bass

```python
import math
import jax
import jax.numpy as jnp
from jax import lax
import numpy as np

D_MODEL = 1024
BATCH = 16
SEQ = 2048
DEPTH = 2

CTX_LEN = 256
GRID_W = 64
EPS = 1e-6
N_BRANCH = 4
BRANCH_W = 512
Q_BLOCK = 128
ROPE_BASE = 10000.0

MLA_HEADS = 8
MLA_NOPE = 64
MLA_ROPE = 32
MLA_V = 64
MLA_Q_RANK = 256
MLA_KV_RANK = 128

NA_HEADS = 8
NA_HEAD_DIM = 64
NA_WIN_R = 8
NA_WIN_C = 16

HY_WIDTH = 512
HY_ORDER = 2
HY_SHORT = 3
HY_EMB = 33
HY_FFN = 64
HY_TARGET = 1e-2
HY_FAST_DECAY_PCT = 0.3
HY_SLOW_DECAY_PCT = 1.5

GLA_HEADS = 4
GLA_DK = 64
GLA_DV = 128
GLA_GATE_RANK = 16
GLA_TAU = 16.0
GLA_CHUNK = 64

IN_LAYOUT = (
    ('mla_cq', MLA_Q_RANK),
    ('mla_ckv', MLA_KV_RANK),
    ('mla_kr', MLA_ROPE),
    ('mla_z', MLA_HEADS * MLA_V),
    ('na_qkv', 3 * NA_HEADS * NA_HEAD_DIM),
    ('na_z', NA_HEADS * NA_HEAD_DIM),
    ('hy_proj', (HY_ORDER + 1) * HY_WIDTH),
    ('hy_z', HY_WIDTH),
    ('gla_qk', 2 * GLA_HEADS * GLA_DK),
    ('gla_v', GLA_HEADS * GLA_DV),
    ('gla_glr', 2 * GLA_GATE_RANK),
    ('gla_z', GLA_HEADS * GLA_DV),
    ('merge', N_BRANCH * D_MODEL),
)
IN_TOTAL = (MLA_Q_RANK + MLA_KV_RANK + MLA_ROPE + MLA_HEADS * MLA_V
            + 4 * NA_HEADS * NA_HEAD_DIM + (HY_ORDER + 2) * HY_WIDTH
            + 2 * GLA_HEADS * GLA_DK + 2 * GLA_HEADS * GLA_DV + 2 * GLA_GATE_RANK
            + N_BRANCH * D_MODEL)
Z_NAMES = ('mla_z', 'na_z', 'hy_z', 'gla_z')

kernel_name = 'hybrid_parallel_mla_na_hyena_gla'


def _col_range(name):
    start = 0
    for n, w in IN_LAYOUT:
        if n == name:
            return start, start + w
        start += w
    raise KeyError(name)


def in_proj(hh, w_in, name):
    a, b = _col_range(name)
    return hh @ w_in[:, a:b]


def rms_norm(x, g):
    xf = x.astype(jnp.float32)
    xf = xf * lax.rsqrt(jnp.mean(jnp.square(xf), axis=-1, keepdims=True) + EPS)
    return (xf * g.astype(jnp.float32)).astype(x.dtype)


def axial_rope(x):
    L, R = x.shape[1], x.shape[-1]
    t = jnp.arange(L)
    row = (t // GRID_W).astype(jnp.float32)
    col = (t % GRID_W).astype(jnp.float32)
    half = R // 2
    inv = jnp.power(ROPE_BASE, -jnp.arange(0, half, 2, dtype=jnp.float32) / half)
    ar, ac = row[:, None] * inv, col[:, None] * inv
    ang = jnp.concatenate([ar, ar, ac, ac], axis=-1)
    shape = (1, L) + (1,) * (x.ndim - 3) + (R,)
    cos = jnp.cos(ang).reshape(shape).astype(x.dtype)
    sin = jnp.sin(ang).reshape(shape).astype(x.dtype)
    a1, b1, a2, b2 = jnp.split(x, 4, axis=-1)
    rx = jnp.concatenate([-b1, a1, -b2, a2], axis=-1)
    return x * cos + rx * sin


def block_attention(q, k, v, scale):
    B, S, H, dq = q.shape
    nb = S // Q_BLOCK
    qb = q.reshape(B, nb, Q_BLOCK, H, dq).transpose(1, 0, 2, 3, 4)

    def one(qblk):
        s = jnp.einsum('bqhd,bkhd->bhqk', qblk, k).astype(jnp.float32) * scale
        p = jax.nn.softmax(s, axis=-1).astype(v.dtype)
        return jnp.einsum('bhqk,bkhd->bqhd', p, v)

    o = lax.map(one, qb)
    return o.transpose(1, 0, 2, 3, 4).reshape(B, S, H * v.shape[-1])


def mla_branch(h, hc, w_in, q_norm, w_uq, kv_norm, w_ukv, need_ctx):
    def queries(hh, rotate):
        B, L, _ = hh.shape
        cq = rms_norm(in_proj(hh, w_in, 'mla_cq'), q_norm)
        q = (cq @ w_uq).reshape(B, L, MLA_HEADS, MLA_NOPE + MLA_ROPE)
        q_nope, q_rot = q[..., :MLA_NOPE], q[..., MLA_NOPE:]
        if rotate:
            q_rot = axial_rope(q_rot)
        return jnp.concatenate([q_nope, q_rot], axis=-1)

    def keys_values(hh, rotate):
        B, L, _ = hh.shape
        ckv = rms_norm(in_proj(hh, w_in, 'mla_ckv'), kv_norm)
        kv = (ckv @ w_ukv).reshape(B, L, MLA_HEADS, MLA_NOPE + MLA_V)
        k_nope, v = kv[..., :MLA_NOPE], kv[..., MLA_NOPE:]
        k_rot = in_proj(hh, w_in, 'mla_kr')[:, :, None, :]
        if rotate:
            k_rot = axial_rope(k_rot)
        k_rot = jnp.broadcast_to(k_rot, (B, L, MLA_HEADS, MLA_ROPE))
        return jnp.concatenate([k_nope, k_rot], axis=-1), v

    scale = (MLA_NOPE + MLA_ROPE) ** -0.5
    kc, vc = keys_values(hc, False)
    k, v = keys_values(h, True)
    y = block_attention(queries(h, True), jnp.concatenate([kc, k], axis=1),
                        jnp.concatenate([vc, v], axis=1), scale)
    yc = block_attention(queries(hc, False), kc, vc, scale) if need_ctx else None
    return y, yc


def neighbourhood_attention(q, k, v, kc, vc, rpb, scale):
    B, S, H, d = q.shape
    rows = S // GRID_W
    wr = min(NA_WIN_R, rows)
    qg = q.reshape(B, rows, GRID_W, H, d)
    kg = k.reshape(B, rows, GRID_W, H, d)
    vg = v.reshape(B, rows, GRID_W, H, d)
    j = jnp.arange(GRID_W)
    c0 = jnp.clip(j - NA_WIN_C // 2, 0, GRID_W - NA_WIN_C)
    col_mask = (j[None, :] >= c0[:, None]) & (j[None, :] < c0[:, None] + NA_WIN_C)
    dcol = jnp.clip(j[None, :] - j[:, None], -(NA_WIN_C - 1), NA_WIN_C - 1) + NA_WIN_C - 1
    rpb = rpb.astype(jnp.float32)
    n_win = wr * GRID_W

    def one_row(r):
        r0 = jnp.clip(r - wr // 2, 0, rows - wr)
        kb = lax.dynamic_slice_in_dim(kg, r0, wr, axis=1)
        vb = lax.dynamic_slice_in_dim(vg, r0, wr, axis=1)
        qr = lax.dynamic_index_in_dim(qg, r, axis=1, keepdims=False)
        drow = r0 + jnp.arange(wr) - r + NA_WIN_R - 1
        bias = rpb[:, drow][:, :, dcol].transpose(0, 2, 1, 3)
        s_win = jnp.einsum('bqhd,brkhd->bhqrk', qr, kb).astype(jnp.float32) * scale + bias
        s_win = jnp.where(col_mask[:, None, :], s_win, -jnp.inf)
        s_ctx = jnp.einsum('bqhd,bkhd->bhqk', qr, kc).astype(jnp.float32) * scale
        s = jnp.concatenate([s_win.reshape(B, H, GRID_W, n_win), s_ctx], axis=-1)
        p = jax.nn.softmax(s, axis=-1).astype(v.dtype)
        p_win = p[..., :n_win].reshape(B, H, GRID_W, wr, GRID_W)
        return (jnp.einsum('bhqrk,brkhd->bqhd', p_win, vb)
                + jnp.einsum('bhqk,bkhd->bqhd', p[..., n_win:], vc))

    o = lax.map(one_row, jnp.arange(rows))
    return o.transpose(1, 0, 2, 3, 4).reshape(B, S, H * d)


def na_branch(h, hc, w_in, rpb, need_ctx):
    def qkv(hh):
        B, L, _ = hh.shape
        u = in_proj(hh, w_in, 'na_qkv').reshape(B, L, 3, NA_HEADS, NA_HEAD_DIM)
        return u[:, :, 0], u[:, :, 1], u[:, :, 2]

    scale = NA_HEAD_DIM ** -0.5
    qc, kc, vc = qkv(hc)
    q, k, v = qkv(h)
    y = neighbourhood_attention(q, k, v, kc, vc, rpb, scale)
    yc = block_attention(qc, kc, vc, scale) if need_ctx else None
    return y, yc


def short_conv(u, w, b):
    y = lax.conv_general_dilated(u, w[:, None, :].astype(u.dtype), window_strides=(1,),
                                 padding=((HY_SHORT // 2, HY_SHORT // 2),),
                                 dimension_numbers=('NWC', 'WIO', 'NWC'),
                                 feature_group_count=u.shape[-1])
    return y + b.astype(u.dtype)


def hyena_filters(L, w1, b1, freq, w2, b2, w3):
    f32 = jnp.float32
    t = jnp.arange(L, dtype=f32)
    t_norm = t / max(L - 1, 1)
    bands = (HY_EMB - 1) // 2
    fr = jnp.linspace(1e-4, bands - 1, bands, dtype=f32)
    ang = (2.0 * math.pi / L) * t[:, None] * fr[None, :]
    z = jnp.concatenate([t_norm[:, None], jnp.cos(ang), -jnp.sin(ang)], axis=-1)
    a = jnp.sin(freq[0].astype(f32) * (z @ w1.astype(f32) + b1.astype(f32)))
    a = jnp.sin(freq[1].astype(f32) * (a @ w2.astype(f32) + b2.astype(f32)))
    filt = (a @ w3.astype(f32)).reshape(L, HY_ORDER, 2, HY_WIDTH)
    max_decay = math.log(HY_TARGET) / HY_FAST_DECAY_PCT
    min_decay = math.log(HY_TARGET) / HY_SLOW_DECAY_PCT
    deltas = jnp.abs(jnp.linspace(min_decay, max_decay, HY_WIDTH, dtype=f32))
    window = jnp.exp(-t_norm[:, None] * deltas[None, :])
    filt = filt * window[:, None, None, :]
    return filt / (jnp.sum(jnp.abs(filt), axis=(0, 2), keepdims=True) + EPS)


def long_conv_bidir(u, h_fwd, h_bwd, skip):
    L, C = u.shape[1], u.shape[2]
    filt2 = jnp.concatenate([h_fwd, jnp.zeros((1, C), jnp.float32), jnp.flip(h_bwd[1:], axis=0)], axis=0)
    uf = u.astype(jnp.float32)
    y = jnp.fft.irfft(jnp.fft.rfft(uf, n=2 * L, axis=1) * jnp.fft.rfft(filt2, axis=0)[None],
                      n=2 * L, axis=1)[:, :L]
    return (y + uf * skip.astype(jnp.float32)).astype(u.dtype)


def hyena_branch(h, hc, w_in, conv_w, conv_b, pe_w1, pe_b1, pe_freq, pe_w2, pe_b2, pe_w3, skip, need_ctx):
    def run(hh):
        L = hh.shape[1]
        u = short_conv(in_proj(hh, w_in, 'hy_proj'), conv_w, conv_b)
        x1, x2, z = jnp.split(u, HY_ORDER + 1, axis=-1)
        filt = hyena_filters(L, pe_w1, pe_b1, pe_freq, pe_w2, pe_b2, pe_w3)
        for n, gate in enumerate((x1, x2)):
            z = gate * long_conv_bidir(z, filt[:, n, 0], filt[:, n, 1], skip[n])
        return z

    return run(h), (run(hc) if need_ctx else None)


def gla_chunked(q, k, v, log_a, s0):
    B, L, H, DK = q.shape
    DV = v.shape[-1]
    C = GLA_CHUNK
    n = L // C

    def chunks(t):
        return t.reshape(B, n, C, H, t.shape[-1]).transpose(1, 0, 3, 2, 4)

    q, k, v, g = chunks(q), chunks(k), chunks(v), chunks(log_a)
    b = jnp.cumsum(g, axis=3)
    b_last = b[:, :, :, -1:]
    b_mid = b[:, :, :, C // 2 - 1:C // 2]
    a = jnp.einsum('nbhtd,nbhsd->nbhts', q * jnp.exp(b - b_mid), k * jnp.exp(b_mid - b))
    a = jnp.where(jnp.tril(jnp.ones((C, C), dtype=bool)), a, 0.0)
    o_intra = jnp.einsum('nbhts,nbhsv->nbhtv', a, v)
    q_in = q * jnp.exp(b)
    k_out = k * jnp.exp(b_last - b)
    decay = jnp.exp(b_last[:, :, :, 0])

    def step(s, inp):
        qi, ko, vi, dec = inp
        o = jnp.einsum('bhtd,bhdv->bhtv', qi, s)
        s = dec[..., None] * s + jnp.einsum('bhtd,bhtv->bhdv', ko, vi)
        return s, o

    s_final, o_inter = lax.scan(step, s0, (q_in, k_out, v, decay))
    o = (o_intra + o_inter).transpose(1, 0, 3, 2, 4).reshape(B, L, H, DV)
    return o, s_final


def gla_branch(h, hc, w_in, wg2, bg, norm_g, need_ctx):
    f32 = jnp.float32

    def feats(hh):
        B, L, _ = hh.shape
        q, k = jnp.split(in_proj(hh, w_in, 'gla_qk').astype(f32), 2, axis=-1)
        q = q.reshape(B, L, GLA_HEADS, GLA_DK) * GLA_DK ** -0.5
        k = k.reshape(B, L, GLA_HEADS, GLA_DK)
        v = in_proj(hh, w_in, 'gla_v').astype(f32).reshape(B, L, GLA_HEADS, GLA_DV)
        glr = in_proj(hh, w_in, 'gla_glr').astype(f32).reshape(B, L, 2, GLA_GATE_RANK)
        logit = jnp.einsum('blir,ire->blie', glr, wg2.astype(f32)) + bg.astype(f32)
        log_a = (jax.nn.log_sigmoid(logit) / GLA_TAU).reshape(B, L, 2, GLA_HEADS, GLA_DK)
        return q, k, v, log_a

    def flip(t):
        return jnp.flip(t, axis=1)

    def bidir(q, k, v, log_a, s_fwd, s_bwd):
        o_f, s_f = gla_chunked(q, k, v, log_a[:, :, 0], s_fwd)
        o_b, s_b = gla_chunked(flip(q), flip(k), flip(v), flip(log_a[:, :, 1]), s_bwd)
        return o_f + flip(o_b), s_f, s_b

    def finish(o, like):
        B, L = o.shape[:2]
        return rms_norm(o, norm_g).reshape(B, L, GLA_HEADS * GLA_DV).astype(like.dtype)

    s0 = jnp.zeros((h.shape[0], GLA_HEADS, GLA_DK, GLA_DV), f32)
    qc, kc, vc, lac = feats(hc)
    o_c, s_f, s_b = bidir(qc, kc, vc, lac, s0, s0)
    q, k, v, la = feats(h)
    o, _, _ = bidir(q, k, v, la, s_f, s_b)
    return finish(o, h), (finish(o_c, hc) if need_ctx else None)


def merge_branches(hh, ys, w_in, w_branch, w_out):
    B, L, _ = hh.shape
    gates = jax.nn.sigmoid(in_proj(hh, w_in, 'merge').astype(jnp.float32)).astype(hh.dtype)
    gates = gates.reshape(B, L, N_BRANCH, D_MODEL)
    terms = [gates[:, :, i] * ((y * jax.nn.silu(in_proj(hh, w_in, zname))) @ w_branch[i])
             for i, (y, zname) in enumerate(zip(ys, Z_NAMES))]
    merged = sum(terms[1:], terms[0])
    return merged @ w_out


def setup_inputs(seed: int = 0) -> dict:
    key = jax.random.key(seed)
    ks = iter(jax.random.split(key, 32))
    f32 = jnp.float32
    D = D_MODEL

    def nrm(shape, scale):
        return jax.random.normal(next(ks), shape, f32) * scale

    def gain(shape):
        return 1.0 + nrm(shape, 0.05)

    return {
        'x': nrm((BATCH, SEQ, D), 1.0),
        'c': nrm((BATCH, D), 1.0),
        'ctx': nrm((BATCH, CTX_LEN, D), 1.0),
        'c_ctx': nrm((D,), 1.0),
        'ada_w': nrm((DEPTH, D, 3 * D), 0.5 * D ** -0.5),
        'ada_b': nrm((DEPTH, 3 * D), 0.01),
        'pre_g': gain((DEPTH, D)),
        'post_g': gain((DEPTH, D)),
        'w_in': nrm((DEPTH, D, IN_TOTAL), D ** -0.5),
        'mla_q_norm': gain((DEPTH, MLA_Q_RANK)),
        'mla_w_uq': nrm((DEPTH, MLA_Q_RANK, MLA_HEADS * (MLA_NOPE + MLA_ROPE)), MLA_Q_RANK ** -0.5),
        'mla_kv_norm': gain((DEPTH, MLA_KV_RANK)),
        'mla_w_ukv': nrm((DEPTH, MLA_KV_RANK, MLA_HEADS * (MLA_NOPE + MLA_V)), MLA_KV_RANK ** -0.5),
        'na_rpb': nrm((DEPTH, NA_HEADS, 2 * NA_WIN_R - 1, 2 * NA_WIN_C - 1), 0.1),
        'hy_conv_w': nrm((DEPTH, HY_SHORT, (HY_ORDER + 1) * HY_WIDTH), HY_SHORT ** -0.5),
        'hy_conv_b': nrm((DEPTH, (HY_ORDER + 1) * HY_WIDTH), 0.01),
        'hy_pe_w1': nrm((DEPTH, HY_EMB, HY_FFN), HY_EMB ** -0.5),
        'hy_pe_b1': nrm((DEPTH, HY_FFN), 0.1),
        'hy_pe_freq': gain((DEPTH, 2, HY_FFN)),
        'hy_pe_w2': nrm((DEPTH, HY_FFN, HY_FFN), HY_FFN ** -0.5),
        'hy_pe_b2': nrm((DEPTH, HY_FFN), 0.1),
        'hy_pe_w3': nrm((DEPTH, HY_FFN, HY_ORDER * 2 * HY_WIDTH), HY_FFN ** -0.5),
        'hy_skip': nrm((DEPTH, HY_ORDER, HY_WIDTH), 1.0),
        'gla_wg2': nrm((DEPTH, 2, GLA_GATE_RANK, GLA_HEADS * GLA_DK), GLA_GATE_RANK ** -0.5),
        'gla_bg': nrm((DEPTH, 2, GLA_HEADS * GLA_DK), 0.1),
        'gla_norm': gain((DEPTH, GLA_DV)),
        'w_branch': nrm((DEPTH, N_BRANCH, BRANCH_W, D), BRANCH_W ** -0.5),
        'w_out': nrm((DEPTH, D, D), D ** -0.5),
    }


def reference(x, c, ctx, c_ctx, ada_w, ada_b, pre_g, post_g, w_in, mla_q_norm, mla_w_uq,
              mla_kv_norm, mla_w_ukv, na_rpb, hy_conv_w, hy_conv_b, hy_pe_w1, hy_pe_b1,
              hy_pe_freq, hy_pe_w2, hy_pe_b2, hy_pe_w3, hy_skip, gla_wg2, gla_bg, gla_norm,
              w_branch, w_out):
    cx = ctx
    for l in range(DEPTH):
        need_ctx = l < DEPTH - 1
        shift, scale, gate = jnp.split(jax.nn.silu(c) @ ada_w[l] + ada_b[l], 3, axis=-1)
        shift_c, scale_c, gate_c = jnp.split(jax.nn.silu(c_ctx) @ ada_w[l] + ada_b[l], 3, axis=-1)
        h = rms_norm(x, pre_g[l]) * (1.0 + scale[:, None]) + shift[:, None]
        hc = rms_norm(cx, pre_g[l]) * (1.0 + scale_c) + shift_c
        w = w_in[l]
        ya, yca = mla_branch(h, hc, w, mla_q_norm[l], mla_w_uq[l], mla_kv_norm[l], mla_w_ukv[l], need_ctx)
        yb, ycb = na_branch(h, hc, w, na_rpb[l], need_ctx)
        yh, ych = hyena_branch(h, hc, w, hy_conv_w[l], hy_conv_b[l], hy_pe_w1[l], hy_pe_b1[l],
                               hy_pe_freq[l], hy_pe_w2[l], hy_pe_b2[l], hy_pe_w3[l], hy_skip[l], need_ctx)
        yg, ycg = gla_branch(h, hc, w, gla_wg2[l], gla_bg[l], gla_norm[l], need_ctx)
        out = merge_branches(h, (ya, yb, yh, yg), w, w_branch[l], w_out[l])
        if need_ctx:
            out_c = merge_branches(hc, (yca, ycb, ych, ycg), w, w_branch[l], w_out[l])
            cx = cx + gate_c * rms_norm(out_c, post_g[l])
        x = x + gate[:, None] * rms_norm(out, post_g[l])
    return x
```

```python
import math
import contextlib
import numpy as np
import ml_dtypes
import concourse.bass as bass
import concourse.mybir as mybir
from concourse.bass_utils import run_bass_kernel_spmd

F32 = mybir.dt.float32
BF16 = mybir.dt.bfloat16
ALU = mybir.AluOpType
AF = mybir.ActivationFunctionType
AX = mybir.AxisListType

D = 1024
TT = 2304
NC = 256
L = 2048
INW = 10688
EPS = 1e-6
C_CQ, C_CKV, C_KR, C_MZ = 0, 256, 384, 416
C_NQ, C_NK, C_NV, C_NZ = 928, 1440, 1952, 2464
C_HP, C_HZ = 2976, 4512
C_GQK, C_GV, C_GLR, C_GZ = 5024, 5536, 6048, 6080
C_MG = 6592
T512 = [(0, 256), (256, 512), (768, 512), (1280, 512), (1792, 512)]
ROT_PERM = list(range(8, 16)) + list(range(0, 8)) + list(range(24, 32)) + list(range(16, 24))


class Sched:
    ENG = ("tensor", "vector", "scalar", "gpsimd", "sync")
    NSLOT = 8

    def __init__(self, nc, stack):
        self.nc = nc
        self.lists = {e: [] for e in self.ENG}
        self.sems = {}
        self.count = {}
        for e in self.ENG:
            self.sems[e] = stack.enter_context(nc.semaphore("c_" + e))
            self.count[e] = 0
        self.slots = {}
        self.slot_cnt = {}
        self.ndma = {}
        for q in ("sync", "gpsimd"):
            self.slots[q] = [stack.enter_context(nc.semaphore("d_%s%d" % (q, i))) for i in range(self.NSLOT)]
            self.slot_cnt[q] = [0] * self.NSLOT
            self.ndma[q] = 0
        self.semobj = {}
        for e in self.ENG:
            self.semobj[("c", e)] = self.sems[e]
        for q in self.slots:
            for i, s in enumerate(self.slots[q]):
                self.semobj[("d", q, i)] = s
        self.waited = {e: {} for e in self.ENG}
        self.trace = {e: [] for e in self.ENG}
        self.lastw = {}
        self.readers = {}

    def _deps(self, r, w):
        deps = {}

        def add(d):
            if d is not None and deps.get(d[0], -1) < d[1]:
                deps[d[0]] = d[1]
        for key in r:
            add(self.lastw.get(key))
        for key in w:
            add(self.lastw.get(key))
            for k, v in self.readers.get(key, {}).items():
                add((k, v))
        return deps

    def _emit_waits(self, eng, deps, same_ok=True):
        for k, v in deps.items():
            if same_ok and k == ("c", eng):
                continue
            if self.waited[eng].get(k, -1) >= v:
                continue
            self.waited[eng][k] = v
            sem = self.semobj[k]
            self.lists[eng].append(lambda e, sem=sem, v=v: e.wait_ge(sem, v))
            self.trace[eng].append(("w", k, v))

    def _record(self, tok, r, w):
        for key in r:
            d = self.readers.setdefault(key, {})
            if d.get(tok[0], -1) < tok[1]:
                d[tok[0]] = tok[1]
        for key in w:
            self.lastw[key] = tok
            self.readers[key] = {}

    def op(self, eng, fn, r=(), w=(), inc=True):
        w = list(w) + [k for k in r if k[0] == "P" and k not in w]
        self._emit_waits(eng, self._deps(r, w), same_ok=(eng == "tensor"))
        sem = self.sems[eng]
        if inc:
            self.count[eng] += 1
            self.lists[eng].append(lambda e, fn=fn, sem=sem: fn(e).then_inc(sem, 1))
            tok = (("c", eng), self.count[eng])
            self.trace[eng].append(("i", ("c", eng), 1))
        else:
            self.lists[eng].append(lambda e, fn=fn: fn(e))
            tok = (("c", eng), self.count[eng] + 1)
        self._record(tok, r, w)

    def dma(self, q, out, in_, r=(), w=()):
        deps = self._deps(r, w)
        i = self.ndma[q]
        self.ndma[q] += 1
        j = i % self.NSLOT
        k = ("d", q, j)
        prev = self.slot_cnt[q][j]
        if prev > 0:
            deps[k] = max(deps.get(k, -1), 16 * prev)
        self._emit_waits(q, deps, same_ok=False)
        self.slot_cnt[q][j] = prev + 1
        sem = self.slots[q][j]
        self.lists[q].append(lambda e, out=out, in_=in_, sem=sem: e.dma_start(out=out, in_=in_).then_inc(sem, 16))
        self.trace[q].append(("i", k, 16))
        self._record((k, 16 * (prev + 1)), r, w)

    def wait_keys(self, eng, keys):
        self._emit_waits(eng, self._deps(keys, ()), same_ok=False)

    def barrier(self):
        tgt = {}
        for e in self.ENG:
            if self.count[e] > 0:
                tgt[("c", e)] = self.count[e]
        for q in self.slots:
            for j in range(self.NSLOT):
                if self.slot_cnt[q][j] > 0:
                    tgt[("d", q, j)] = 16 * self.slot_cnt[q][j]
        for e in self.ENG:
            self._emit_waits(e, dict(tgt), same_ok=True)

    def simulate(self):
        val = {}
        pc = {e: 0 for e in self.ENG}
        prog = True
        while prog:
            prog = False
            for e in self.ENG:
                t = self.trace[e]
                while pc[e] < len(t):
                    kind, k, v = t[pc[e]]
                    if kind == "w":
                        if val.get(k, 0) < v:
                            break
                    else:
                        val[k] = val.get(k, 0) + v
                    pc[e] += 1
                    prog = True
        stuck = {e: (pc[e], len(self.trace[e]), self.trace[e][pc[e]], val.get(self.trace[e][pc[e]][1], 0))
                 for e in self.ENG if pc[e] < len(self.trace[e])}
        return stuck

    def finish(self):
        with self.nc.Block() as block:
            @block.tensor
            def _(e):
                for f in self.lists["tensor"]:
                    f(e)

            @block.vector
            def _(e):
                for f in self.lists["vector"]:
                    f(e)

            @block.scalar
            def _(e):
                for f in self.lists["scalar"]:
                    f(e)

            @block.gpsimd
            def _(e):
                for f in self.lists["gpsimd"]:
                    f(e)

            @block.sync
            def _(e):
                for f in self.lists["sync"]:
                    f(e)


class KB:
    def __init__(self, nc, st, dbg):
        self.nc = nc
        self.st = st
        self.S = Sched(nc, st)
        self.dbg = dbg
        self.uid = 0
        self.evi = 0
        self.din = {}
        self.pbank = []
        for i in range(3):
            t = st.enter_context(nc.psum_tensor("p%d" % i, [128, 512], F32))
            self.pbank.append((t, "Pp%d" % i))
        self.q2 = []
        for i in range(2):
            t = st.enter_context(nc.psum_tensor("q%d" % i, [128, 1024], F32))
            self.q2.append(t)
        self.pool = [(t[:, :], k) for t, k in self.pbank[:2]]
        for i in range(2):
            self.pool.append((self.q2[i][:, 0:512], "Pq%da" % i))
            self.pool.append((self.q2[i][:, 512:1024], "Pq%db" % i))
        self.pi = 0
        self.ptb = st.enter_context(nc.psum_tensor("ptb", [128, 1024], BF16))

    def inp(self, name, shape, dt=F32):
        ap = self.nc.dram_tensor(name, list(shape), dt, kind="ExternalInput").ap()
        self.din[name] = ap
        return ap

    def scratch(self, name, shape, dt):
        kind = "ExternalOutput" if (self.dbg and name in self.dbg) else "Internal"
        return self.nc.dram_tensor(name, list(shape), dt, kind=kind).ap()

    def dump(self, name, ap, shape, dt, keys):
        if not self.dbg or name not in self.dbg:
            return
        d = self.nc.dram_tensor("dbg_" + name, list(shape), dt, kind="ExternalOutput").ap()
        self.S.dma("sync", d, ap, r=keys, w=["dbg_" + name])

    def sb(self, stack, name, shape, dt):
        self.uid += 1
        return stack.enter_context(self.nc.sbuf_tensor("%s_%d" % (name, self.uid), list(shape), dt))

    def ps(self):
        ap, k = self.pool[self.pi % len(self.pool)]
        self.pi += 1
        return ap, k

    def ev(self):
        self.evi += 1
        return "scalar" if self.evi % 2 else "vector"

    def mm(self, out, lhsT, rhs, start, stop, r, w, sg=False, inc=None):
        inc = stop if inc is None else inc
        if sg:
            self.S.op("tensor", lambda e: e.matmul(out, lhsT, rhs, start=start, stop=stop, skip_group_check=True), r=r, w=w, inc=inc)
        else:
            self.S.op("tensor", lambda e: e.matmul(out, lhsT, rhs, start=start, stop=stop), r=r, w=w, inc=inc)

    def copy(self, eng, out, in_, r, w, scale=None):
        if eng == "scalar":
            if scale is None:
                self.S.op("scalar", lambda e: e.activation(out=out, in_=in_, func=AF.Copy), r=r, w=w)
            else:
                self.S.op("scalar", lambda e: e.activation(out=out, in_=in_, func=AF.Copy, scale=scale), r=r, w=w)
        else:
            if scale is None:
                self.S.op(eng, lambda e: e.tensor_copy(out=out, in_=in_), r=r, w=w)
            else:
                self.S.op(eng, lambda e: e.tensor_scalar(out=out, in0=in_, scalar1=scale, scalar2=None, op0=ALU.mult), r=r, w=w)

    def act(self, out, in_, func, r, w, **kw):
        self.S.op("scalar", lambda e: e.activation(out=out, in_=in_, func=func, **kw), r=r, w=w)

    def tt(self, out, in0, in1, op, r, w, eng="vector"):
        self.S.op(eng, lambda e: e.tensor_tensor(out=out, in0=in0, in1=in1, op=op), r=r, w=w)

    def ts(self, out, in0, s1, s2, op0, op1, r, w, eng="vector"):
        if op1 is None:
            self.S.op(eng, lambda e: e.tensor_scalar(out=out, in0=in0, scalar1=s1, scalar2=None, op0=op0), r=r, w=w)
        else:
            self.S.op(eng, lambda e: e.tensor_scalar(out=out, in0=in0, scalar1=s1, scalar2=s2, op0=op0, op1=op1), r=r, w=w)

    def stt(self, out, in0, scalar, in1, op0, op1, r, w, eng="vector"):
        self.S.op(eng, lambda e: e.scalar_tensor_tensor(out=out, in0=in0, scalar=scalar, in1=in1, op0=op0, op1=op1), r=r, w=w)

    def memset(self, ap, val, w, eng="vector"):
        self.S.op(eng, lambda e: e.memset(ap, val), r=(), w=w)

    def rstd(self, ssq_ps, pk, out, ok, n, dim, parts=128):
        self.act(out[0:parts, 0:n], ssq_ps[0:parts, 0:n], AF.Ln, r=[pk], w=[ok], scale=1.0 / dim, bias=self.epsc[0:parts, 0:1])
        self.act(out[0:parts, 0:n], out[0:parts, 0:n], AF.Exp, r=[ok], w=[ok], scale=-0.5)

    def load_w(self, dst, key, l, c0, c1):
        src = self.w_in[l].rearrange("(k p) c -> p k c", p=128)[:, :, c0:c1]
        self.S.dma("gpsimd", dst, src, w=[key])

    def proj_fm(self, wt, wkey, c0, m, t0, n):
        ps, pk = self.ps()
        for k in range(8):
            self.mm(ps[0:m, 0:n], wt[:, k, c0:c0 + m], self.hT[:, k, t0:t0 + n], k == 0, k == 7, r=[wkey, "hT"], w=[pk])
        return ps, pk

    def proj_tm(self, wt, wkey, c0, ncols, t0, m):
        ps, pk = self.ps()
        for k in range(8):
            self.mm(ps[0:m, 0:ncols], self.hT[:, k, t0:t0 + m], wt[:, k, c0:c0 + ncols], k == 0, k == 7, r=[wkey, "hT"], w=[pk])
        return ps, pk


def build(dbg=None, layers=(0, 1), batches=(0, 1), branches="abhgm"):
    nc = bass.Bass("TRN2", target_bir_lowering=False)
    st = contextlib.ExitStack()
    with st:
        K = KB(nc, st, dbg)
        S = K.S
        xT = K.inp("xT", [2, D, TT])
        cT = K.inp("cT", [128, 8, 3])
        ada_w = K.inp("ada_w", [2, D, 3 * D])
        ada_bT = K.inp("ada_bT", [2, 128, 24])
        pre_gT = K.inp("pre_gT", [2, 128, 8])
        post_gT = K.inp("post_gT", [2, 128, 8])
        K.w_in = K.inp("w_in", [2, D, INW])
        w_kr_sw = K.inp("w_kr_sw", [2, D, 32])
        qnT = K.inp("mla_q_normT", [2, 128, 2])
        kvnT = K.inp("mla_kv_normT", [2, 128, 1])
        w_uq = K.inp("mla_w_uq", [2, 256, 768])
        w_uq_sw = K.inp("mla_w_uq_sw", [2, 256, 768])
        w_ukv = K.inp("mla_w_ukv", [2, 128, 1024])
        na_bias = K.inp("na_bias", [2, 64, 8, 15 * 64])
        HA = {}
        for nm, shp in (("hy_conv_wT", [2, 128, 12, 3]), ("hy_conv_bT", [2, 128, 12]), ("hy_skipT", [2, 128, 2, 4]),
                        ("hy_pe_w1", [2, 33, 64]), ("hy_pe_w2", [2, 64, 64]), ("hy_pe_w3", [2, 64, 2048]),
                        ("hy_pe_bT", [2, 64, 2]), ("hy_pe_freqT", [2, 64, 2]), ("zposL", [33, L]), ("zposC", [33, NC]),
                        ("ntnL", [128, 16]), ("ntnC", [128, 2]), ("c_delta", [128, 512]), ("c_identf", [128, 128])):
            HA[nm] = K.inp(nm, shp)
        for sfx, n_ in (("L", L), ("C", NC)):
            for nm in ("dC", "dS", "dCt", "dSt"):
                HA[nm + sfx] = K.inp(nm + sfx, [n_, n_], BF16)
            HA["Hsp" + sfx] = K.scratch("Hsp" + sfx, [2, 2, 2, n_, 512], F32)
        gla_wgblk = K.inp("gla_wgblk", [2, 33, 512])
        gla_normR = K.inp("gla_normR", [2, 128, 128])
        c_tri = K.inp("c_tri", [128, 6, 128])
        c_amask = K.inp("c_amask", [64, 2, 64])
        c_onesrow = K.inp("c_onesrow", [1, TT], BF16)
        w_branch = K.inp("w_branch", [2, 4, 512, D])
        w_out = K.inp("w_out", [2, D, D])
        c_ident = K.inp("c_ident", [128, 128], BF16)
        c_ones = K.inp("c_ones", [128, 128], BF16)
        c_ropec = K.inp("c_ropec", [128, TT])
        c_ropes = K.inp("c_ropes", [128, TT])
        c_namask = K.inp("c_namask", [64, 15 * 64])
        out = nc.dram_tensor("outT", [2, D, L], F32, kind="ExternalOutput").ap()
        xs1 = K.scratch("xs1", [2, D, TT], F32)
        yzs = K.scratch("yzs", [4, 512, TT], BF16)

        ident = K.sb(st, "ident", [128, 128], BF16)
        ones = K.sb(st, "ones", [128, 128], BF16)
        K.epsc = K.sb(st, "epsc", [128, 1], F32)
        S.dma("sync", ident[:], c_ident, w=["ident"])
        S.dma("sync", ones[:], c_ones, w=["ones"])
        K.memset(K.epsc[:], EPS, w=["epsc"])
        K.hT = K.sb(st, "hT", [128, 8, TT], BF16)
        hT = K.hT
        modA = [K.sb(st, "modA", [128, 8, 3], F32) for _ in range(2)]
        modB = [K.sb(st, "modB", [128, 8, 3], F32) for _ in range(2)]
        modG = [K.sb(st, "modG", [128, 8, 3], F32) for _ in range(2)]

        with contextlib.ExitStack() as ph:
            cTs = K.sb(ph, "cTs", [128, 8, 3], F32)
            cs = K.sb(ph, "cs", [128, 8, 3], BF16)
            S.dma("sync", cTs[:], cT, w=["cTs"])
            K.act(cs[:], cTs[:], AF.Silu, r=["cTs"], w=["cs"])
            wb = [K.sb(ph, "adaw", [128, 8, 512], BF16) for _ in range(2)]
            adab = K.sb(ph, "adab", [128, 24], F32)
            preg = K.sb(ph, "preg", [128, 8], F32)
            postg = K.sb(ph, "postg", [128, 8], F32)
            mod = K.sb(ph, "mod", [128, 24, 3], F32)
            for l in layers:
                S.dma("sync", adab[:], ada_bT[l], w=["adab"])
                S.dma("sync", preg[:], pre_gT[l], w=["preg"])
                S.dma("sync", postg[:], post_gT[l], w=["postg"])
                for piece in range(6):
                    wt = wb[piece % 2]
                    wk = "adaw%d" % (piece % 2)
                    S.dma("gpsimd", wt[:], ada_w[l].rearrange("(k p) c -> p k c", p=128)[:, :, piece * 512:(piece + 1) * 512], w=[wk])
                    for j in range(4):
                        ch = piece * 4 + j
                        ps, pk = K.ps()
                        for k in range(8):
                            K.mm(ps[:, 0:3], wt[:, k, j * 128:(j + 1) * 128], cs[:, k, :], k == 0, k == 7, r=[wk, "cs"], w=[pk])
                        K.ts(mod[:, ch, :], ps[:, 0:3], adab[:, ch:ch + 1], None, ALU.add, None, r=[pk, "adab"], w=["mod"])
                for k in range(8):
                    K.ts(modA[l][:, k, :], mod[:, 8 + k, :], 1.0, preg[:, k:k + 1], ALU.add, ALU.mult, r=["mod", "preg"], w=["modA%d" % l])
                    K.copy("vector", modB[l][:, k, :], mod[:, k, :], r=["mod"], w=["modB%d" % l])
                    K.ts(modG[l][:, k, :], mod[:, 16 + k, :], postg[:, k:k + 1], None, ALU.mult, None, r=["mod", "postg"], w=["modG%d" % l])
            S.barrier()
        HA.update(dict(ones=ones, ident=ident, yzs=yzs))
        if "h" in branches:
            for l in layers:
                hyena_filters(K, l, L, HA)
                if l < 1:
                    hyena_filters(K, l, NC, HA)

        for l in layers:
            need_ctx = l < 1
            for b in batches:
                xsrc = (xT[b] if l == 0 else xs1[b]).rearrange("(k p) t -> p k t", p=128)
                if l == 0:
                    xdst = xs1[b].rearrange("(k p) t -> p k t", p=128)
                    doff = 0
                else:
                    xdst = out[b].rearrange("(k p) t -> p k t", p=128)
                    doff = -NC
                with contextlib.ExitStack() as ph:
                    xb = [K.sb(ph, "xb", [128, 8, 512], F32) for _ in range(2)]
                    sq = [K.sb(ph, "sq", [128, 8, 512], BF16) for _ in range(2)]
                    rs = [K.sb(ph, "rs", [128, 512], F32) for _ in range(2)]
                    for ti, (t0, n) in enumerate(T512):
                        col = 2 if t0 < NC else b
                        x_, xk = xb[ti % 2], "xb%d" % (ti % 2)
                        s_, sk = sq[ti % 2], "sq%d" % (ti % 2)
                        r_, rk = rs[ti % 2], "rs%d" % (ti % 2)
                        S.dma("sync", x_[:, :, 0:n], xsrc[:, :, t0:t0 + n], w=[xk])
                        K.act(s_[:, :, 0:n], x_[:, :, 0:n], AF.Square, r=[xk], w=[sk])
                        ps, pk = K.ps()
                        for k in range(8):
                            K.mm(ps[:, 0:n], ones[:, :], s_[:, k, 0:n], k == 0, k == 7, r=["ones", sk], w=[pk])
                        K.rstd(ps, pk, r_, rk, n, D)
                        for k in range(8):
                            K.tt(x_[:, k, 0:n], x_[:, k, 0:n], r_[:, 0:n], ALU.mult, r=[xk, rk], w=[xk], eng="gpsimd")
                            K.ts(hT[:, k, t0:t0 + n], x_[:, k, 0:n], modA[l][:, k, col:col + 1], modB[l][:, k, col:col + 1],
                                 ALU.mult, ALU.add, r=[xk, "modA%d" % l, "modB%d" % l], w=["hT"])
                    S.barrier()

                if "a" in branches:
                    mla_phase(K, l, b, need_ctx, dict(w_kr_sw=w_kr_sw, qnT=qnT, kvnT=kvnT, w_uq=w_uq, w_uq_sw=w_uq_sw,
                                                      w_ukv=w_ukv, ropec=c_ropec, ropes=c_ropes, ident=ident, ones=ones, yzs=yzs))
                if "b" in branches:
                    na_phase(K, l, b, need_ctx, dict(na_bias=na_bias, namask=c_namask, ident=ident, yzs=yzs))
                if "h" in branches:
                    hyena_phase(K, l, b, need_ctx, HA, L, NC)
                    if need_ctx:
                        hyena_phase(K, l, b, need_ctx, HA, NC, 0)
                if "g" in branches:
                    gla_phase(K, l, b, need_ctx, dict(gla_wgblk=gla_wgblk, gla_normR=gla_normR, c_tri=c_tri, c_amask=c_amask,
                                                      c_onesrow=c_onesrow, ident=ident, yzs=yzs))
                if "m" in branches:
                    merge_phase(K, l, b, need_ctx, dict(w_branch=w_branch, w_out=w_out, ones=ones, yzs=yzs, xsrc=xsrc,
                                                        xdst=xdst, doff=doff, modG=modG))
        S.wait_keys("sync", ["outdone"])
        S.barrier()
        S.finish()
    return nc, K


def mla_phase(K, l, b, need_ctx, A):
    S = K.S
    hT = K.hT
    ident, ones = A["ident"], A["ones"]
    scale = 96 ** -0.5
    with contextlib.ExitStack() as ph:
        Wcq = K.sb(ph, "Wcq", [128, 8, 256], BF16)
        Wckv = K.sb(ph, "Wckv", [128, 8, 128], BF16)
        Wkr = K.sb(ph, "Wkr", [128, 8, 96], BF16)
        Wkrs = K.sb(ph, "Wkrs", [128, 8, 96], BF16)
        Wz = K.sb(ph, "Wz", [128, 8, 512], BF16)
        wuq = K.sb(ph, "wuq", [128, 2, 768], BF16)
        wuqs = K.sb(ph, "wuqs", [128, 2, 768], BF16)
        wkn = K.sb(ph, "wkn", [128, 8, 64], BF16)
        wv = K.sb(ph, "wv", [128, 8, 64], BF16)
        qn = K.sb(ph, "qn", [128, 2], F32)
        kvn = K.sb(ph, "kvn", [128, 1], F32)
        cqn = K.sb(ph, "cqn", [128, 2, TT], BF16)
        ckvn = K.sb(ph, "ckvn", [128, TT], BF16)
        krT = K.sb(ph, "krT", [128, TT], BF16)
        cosT = K.sb(ph, "cosT", [128, TT], F32)
        sinT = K.sb(ph, "sinT", [128, TT], F32)
        V1 = K.sb(ph, "V1", [128, 18, 8, 65], BF16)
        KT = [K.sb(ph, "KT", [128, TT], BF16) for _ in range(2)]
        QT = [K.sb(ph, "QT", [128, TT], BF16) for _ in range(2)]
        ya = K.sb(ph, "ya", [128, 18, 512], F32)
        pT = [K.sb(ph, "pT", [128, 512], BF16) for _ in range(4)]
        tf = [K.sb(ph, "tf", [128, 512], F32) for _ in range(4)]
        tb = [K.sb(ph, "tbf", [128, 512], BF16) for _ in range(2)]
        rec = K.sb(ph, "rec", [128, 8], F32)
        yzb = [K.sb(ph, "yzb", [128, 512], BF16) for _ in range(2)]
        yzT = K.sb(ph, "yzT", [128, 4, TT], BF16)

        K.load_w(Wcq[:], "Wcq", l, C_CQ, C_CQ + 256)
        K.load_w(Wckv[:], "Wckv", l, C_CKV, C_CKV + 128)
        K.load_w(Wz[:], "Wz", l, C_MZ, C_MZ + 512)
        K.memset(Wkr[:], 0.0, w=["Wkr"])
        K.memset(Wkrs[:], 0.0, w=["Wkrs"])
        K.load_w(Wkr[:, :, 64:96], "Wkr", l, C_KR, C_KR + 32)
        S.dma("gpsimd", Wkrs[:, :, 64:96], A["w_kr_sw"][l].rearrange("(k p) c -> p k c", p=128), w=["Wkrs"])
        S.dma("gpsimd", wuq[:], A["w_uq"][l].rearrange("(k p) c -> p k c", p=128), w=["wuq"])
        S.dma("gpsimd", wuqs[:], A["w_uq_sw"][l].rearrange("(k p) c -> p k c", p=128), w=["wuqs"])
        ukv = A["w_ukv"][l].rearrange("k (h t d) -> k h t d", t=2, d=64)
        S.dma("gpsimd", wkn[:], ukv[:, :, 0, :], w=["wkn"])
        S.dma("gpsimd", wv[:], ukv[:, :, 1, :], w=["wv"])
        S.dma("sync", qn[:], A["qnT"][l], w=["qn"])
        S.dma("sync", kvn[:], A["kvnT"][l], w=["kvn"])
        S.dma("sync", cosT[64:96, :], A["ropec"][64:96, :], w=["cosT"])
        S.dma("sync", sinT[64:96, :], A["ropes"][64:96, :], w=["sinT"])
        K.memset(ya[:, 0, 0:144], 1.0, w=["ya"])
        K.copy("vector", V1[:, :, :, 64:65], ya[:, 0, 0:144].rearrange("p (a b c) -> p a b c", b=8, c=1), r=["ya"], w=["V1"])

        for ti, (t0, n) in enumerate(T512):
            pss = []
            for j in range(2):
                ps, pk = K.proj_fm(Wcq, "Wcq", j * 128, 128, t0, n)
                f_, fk = tf[j], "tf%d" % j
                K.copy("scalar", f_[:, 0:n], ps[:, 0:n], r=[pk], w=[fk])
                K.tt(tb[j][:, 0:n], f_[:, 0:n], f_[:, 0:n], ALU.mult, r=[fk], w=["tb%d" % j])
            ps, pk = K.ps()
            for j in range(2):
                K.mm(ps[:, 0:n], ones[:, :], tb[j][:, 0:n], j == 0, j == 1, r=["ones", "tb%d" % j], w=[pk])
            K.rstd(ps, pk, tf[2], "tf2", n, 256)
            for j in range(2):
                K.stt(cqn[:, j, t0:t0 + n], tf[j][:, 0:n], qn[:, j:j + 1], tf[2][:, 0:n], ALU.mult, ALU.mult,
                      r=["tf%d" % j, "qn", "tf2"], w=["cqn"])
            ps, pk = K.proj_fm(Wckv, "Wckv", 0, 128, t0, n)
            K.copy("scalar", tf[0][:, 0:n], ps[:, 0:n], r=[pk], w=["tf0"])
            K.tt(tb[0][:, 0:n], tf[0][:, 0:n], tf[0][:, 0:n], ALU.mult, r=["tf0"], w=["tb0"])
            ps, pk = K.ps()
            K.mm(ps[:, 0:n], ones[:, :], tb[0][:, 0:n], True, True, r=["ones", "tb0"], w=[pk])
            K.rstd(ps, pk, tf[2], "tf2", n, 128)
            K.stt(ckvn[:, t0:t0 + n], tf[0][:, 0:n], kvn[:, 0:1], tf[2][:, 0:n], ALU.mult, ALU.mult,
                  r=["tf0", "kvn", "tf2"], w=["ckvn"])
            ps1, pk1 = K.proj_fm(Wkr, "Wkr", 0, 96, t0, n)
            ps2, pk2 = K.proj_fm(Wkrs, "Wkrs", 0, 96, t0, n)
            K.tt(tf[0][64:96, 0:n], ps1[64:96, 0:n], cosT[64:96, t0:t0 + n], ALU.mult, r=[pk1, "cosT"], w=["tf0"])
            K.tt(tf[1][64:96, 0:n], ps2[64:96, 0:n], sinT[64:96, t0:t0 + n], ALU.mult, r=[pk2, "sinT"], w=["tf1"])
            K.tt(krT[64:96, t0:t0 + n], tf[0][64:96, 0:n], tf[1][64:96, 0:n], ALU.add, r=["tf0", "tf1"], w=["krT"])
        import os
        STG = int(os.environ.get("MLA_STAGE", "9"))
        if STG < 1:
            S.barrier()
            return
        for i in range(2):
            K.copy("vector", KT[i][64:96, :], krT[64:96, :], r=["krT"], w=["KT%d" % i])
        for kt in range(18):
            ps, pk = K.ps()
            K.mm(ps[:, 0:512], ckvn[:, kt * 128:(kt + 1) * 128], wv[:].rearrange("p h d -> p (h d)"), True, True, r=["ckvn", "wv"], w=[pk])
            K.copy(K.ev(), V1[:, kt, :, 0:64], ps[:, 0:512].rearrange("p (h d) -> p h d", d=64), r=[pk], w=["V1"])
        if STG < 2:
            S.barrier()
            return
        qtiles = ([(0, 256, 2)] if need_ctx else []) + [(256 + 512 * j, 512, 18) for j in range(4)]
        pi = 0
        for h in range(8):
            KT_, kk = KT[h % 2], "KT%d" % (h % 2)
            QT_, qk = QT[h % 2], "QT%d" % (h % 2)
            for ti, (t0, n) in enumerate(T512):
                ps, pk = K.ps()
                K.mm(ps[0:64, 0:n], wkn[:, h, :], ckvn[:, t0:t0 + n], True, True, r=["wkn", "ckvn"], w=[pk])
                K.copy(K.ev(), KT_[0:64, t0:t0 + n], ps[0:64, 0:n], r=[pk], w=[kk])
                if t0 < NC and not need_ctx:
                    continue
                ps1, pk1 = K.ps()
                for j in range(2):
                    K.mm(ps1[0:96, 0:n], wuq[:, j, h * 96:(h + 1) * 96], cqn[:, j, t0:t0 + n], j == 0, j == 1, r=["wuq", "cqn"], w=[pk1])
                ps2, pk2 = K.ps()
                for j in range(2):
                    K.mm(ps2[0:96, 0:n], wuqs[:, j, h * 96:(h + 1) * 96], cqn[:, j, t0:t0 + n], j == 0, j == 1, r=["wuqs", "cqn"], w=[pk2])
                K.copy("scalar", QT_[0:64, t0:t0 + n], ps1[0:64, 0:n], r=[pk1], w=[qk])
                K.tt(tf[0][64:96, 0:n], ps1[64:96, 0:n], cosT[64:96, t0:t0 + n], ALU.mult, r=[pk1, "cosT"], w=["tf0"])
                K.tt(tf[1][64:96, 0:n], ps2[64:96, 0:n], sinT[64:96, t0:t0 + n], ALU.mult, r=[pk2, "sinT"], w=["tf1"])
                K.tt(QT_[64:96, t0:t0 + n], tf[0][64:96, 0:n], tf[1][64:96, 0:n], ALU.add, r=["tf0", "tf1"], w=[qk])
            if STG < 3:
                continue
            for (q0, nq, nkt) in qtiles:
                po, pok = K.pbank[2]
                nj = nq // 128
                for kt in range(nkt):
                    ps, pk = K.ps()
                    K.mm(ps[:, 0:nq], KT_[0:96, kt * 128:(kt + 1) * 128], QT_[0:96, q0:q0 + nq], True, True, r=[kk, qk], w=[pk])
                    p_, ppk = pT[pi % 4], "pT%d" % (pi % 4)
                    pi += 1
                    K.act(p_[:, 0:nq], ps[:, 0:nq], AF.Exp, r=[pk], w=[ppk], scale=scale)
                    for j in range(nj):
                        K.mm(po[:, j * 65:(j + 1) * 65], p_[:, j * 128:(j + 1) * 128], V1[:, kt, h, :], kt == 0 and j == 0, kt == nkt - 1,
                             r=[ppk, "V1"], w=[pok], sg=True)
                for j in range(nj):
                    ti = (q0 + j * 128) // 128
                    K.S.op("vector", lambda e, j=j: e.reciprocal(out=rec[:, j:j + 1], in_=po[:, j * 65 + 64:j * 65 + 65]), r=[pok], w=["rec"])
                    K.ts(ya[:, ti, h * 64:(h + 1) * 64], po[:, j * 65:j * 65 + 64], rec[:, j:j + 1], None, ALU.mult, None,
                         r=[pok, "rec"], w=["ya"])
        if STG < 4:
            S.barrier()
            return
        K.dump("cqn", cqn[:], [128, 2, TT], BF16, ["cqn"])
        K.dump("ckvn", ckvn[:], [128, TT], BF16, ["ckvn"])
        K.dump("krT", krT[64:96, :], [32, TT], BF16, ["krT"])
        K.dump("V1", V1[:], [128, 18, 8, 65], BF16, ["V1"])
        K.dump("QT", QT[1][0:96, :], [96, TT], BF16, ["QT1"])
        K.dump("KT", KT[1][0:96, :], [96, TT], BF16, ["KT1"])
        K.dump("ya", ya[:], [128, 18, 512], F32, ["ya"])
        for i in range(18):
            if i < 2 and not need_ctx:
                continue
            ps, pk = K.proj_tm(Wz, "Wz", 0, 512, i * 128, 128)
            K.act(tf[i % 2][:, :], ps[:, 0:512], AF.Silu, r=[pk], w=["tf%d" % (i % 2)])
            K.tt(yzb[i % 2][:, :], ya[:, i, :], tf[i % 2][:, :], ALU.mult, r=["ya", "tf%d" % (i % 2)], w=["yzb%d" % (i % 2)])
            if STG < 5:
                continue
            for c in range(4):
                pt = K.ptb[:, c * 128:c * 128 + 128]
                K.S.op("tensor", lambda e, pt=pt, i=i, c=c: e.transpose(pt, yzb[i % 2][:, c * 128:(c + 1) * 128], ident[:, :]),
                       r=["yzb%d" % (i % 2), "ident"], w=["Pptb"])
            if STG < 6:
                continue
            K.copy(K.ev(), yzT[:, :, i * 128:(i + 1) * 128], K.ptb[:, 0:512].rearrange("p (c t) -> p c t", t=128), r=["Pptb"], w=["yzT"])
        if STG >= 7:
            c0_ = 0 if need_ctx else NC
            S.dma("sync", A["yzs"][0].rearrange("(c p) t -> p c t", p=128)[:, :, c0_:TT], yzT[:, :, c0_:TT], r=["yzT"], w=["yzs0"])
        S.barrier()


def na_phase(K, l, b, need_ctx, A):
    S = K.S
    hT = K.hT
    ident = A["ident"]
    with contextlib.ExitStack() as ph:
        Wq = K.sb(ph, "Wq", [128, 8, 512], BF16)
        Wk = K.sb(ph, "Wk", [128, 8, 512], BF16)
        Wv = K.sb(ph, "Wv", [128, 8, 512], BF16)
        Wz = K.sb(ph, "Wz", [128, 8, 512], BF16)
        qT = K.sb(ph, "qT", [128, 4, TT], BF16)
        kT = K.sb(ph, "kT", [128, 4, TT], BF16)
        Vr = K.sb(ph, "Vr", [128, 36, 8, 65], BF16)
        biasm = K.sb(ph, "biasm", [128, 8, 15 * 64], BF16)
        bt = [K.sb(ph, "bt", [128, 15 * 64], F32) for _ in range(2)]
        mk = K.sb(ph, "mk", [128, 15 * 64], F32)
        pT = [K.sb(ph, "pT", [128, 768], BF16) for _ in range(3)]
        rec = K.sb(ph, "rec", [128, 8], F32)
        ybr = [K.sb(ph, "ybr", [128, 512], F32) for _ in range(2)]
        sz = [K.sb(ph, "sz", [128, 512], F32) for _ in range(2)]
        yzb = [K.sb(ph, "yzb", [128, 512], BF16) for _ in range(2)]
        yzT = K.sb(ph, "yzT", [128, 4, TT], BF16)
        K.load_w(Wq[:], "Wq", l, C_NQ, C_NQ + 512)
        K.load_w(Wk[:], "Wk", l, C_NK, C_NK + 512)
        K.load_w(Wv[:], "Wv", l, C_NV, C_NV + 512)
        K.load_w(Wz[:], "Wz", l, C_NZ, C_NZ + 512)
        S.dma("sync", mk[0:64, :], A["namask"], w=["mk"])
        K.memset(bt[0][:, 0:288], 1.0, w=["bt0"])
        K.copy("vector", Vr[:, :, :, 64:65], bt[0][:, 0:288].rearrange("p (a b c) -> p a b c", b=8, c=1), r=["bt0"], w=["Vr"])
        for h in range(8):
            S.dma("sync", bt[h % 2][0:64, :], A["na_bias"][l][:, h, :], w=["bt%d" % (h % 2)])
            K.stt(biasm[0:64, h, :], bt[h % 2][0:64, :], 8.0, mk[0:64, :], ALU.mult, ALU.add, r=["bt%d" % (h % 2), "mk"], w=["biasm"])
        for ti, (t0, n) in enumerate(T512):
            for c in range(4):
                if not (t0 < NC and not need_ctx):
                    ps, pk = K.proj_fm(Wq, "Wq", c * 128, 128, t0, n)
                    K.copy(K.ev(), qT[:, c, t0:t0 + n], ps[:, 0:n], r=[pk], w=["qT"])
                ps, pk = K.proj_fm(Wk, "Wk", c * 128, 128, t0, n)
                K.copy(K.ev(), kT[:, c, t0:t0 + n], ps[:, 0:n], r=[pk], w=["kT"])
        for r in range(36):
            ps, pk = K.proj_tm(Wv, "Wv", 0, 512, r * 64, 64)
            K.copy(K.ev(), Vr[0:64, r, :, 0:64], ps[0:64, 0:512].rearrange("p (h d) -> p h d", d=64), r=[pk], w=["Vr"])
        it = 0
        for r in range(36):
            if r < 4 and not need_ctx:
                continue
            if r < 4:
                keyrows, nwin = [0, 1, 2, 3], 0
                drow0 = 0
            else:
                g = r - 4
                r0 = min(max(g - 4, 0), 24)
                keyrows = [4 + r0 + i for i in range(8)] + [0, 1, 2, 3]
                nwin = 8
                drow0 = r0 - g + 7
            nk = len(keyrows)
            for h in range(8):
                c, pb = h // 2, (h % 2) * 64
                psS = K.q2[it % 2]
                sk = ["Pq%da" % (it % 2), "Pq%db" % (it % 2)]
                for i, kr in enumerate(keyrows):
                    K.mm(psS[0:64, i * 64:(i + 1) * 64], kT[pb:pb + 64, c, kr * 64:(kr + 1) * 64], qT[pb:pb + 64, c, r * 64:(r + 1) * 64],
                         i == 0 or i == 8, i >= nwin, r=["kT", "qT"], w=[sk[i // 8]], sg=True)
                if nwin:
                    K.mm(psS[0:64, 0:512], ident[0:64, 0:64], biasm[0:64, h, drow0 * 64:(drow0 + 8) * 64], False, True,
                         r=["ident", "biasm"], w=[sk[0]], sg=True)
                p_, ppk = pT[it % 3], "pT%d" % (it % 3)
                it += 1
                if nwin:
                    K.act(p_[0:64, 0:512], psS[0:64, 0:512], AF.Exp, r=[sk[0]], w=[ppk], scale=0.125)
                    K.act(p_[0:64, 512:768], psS[0:64, 512:768], AF.Exp, r=[sk[1]], w=[ppk], scale=0.125)
                else:
                    K.act(p_[0:64, 0:256], psS[0:64, 0:256], AF.Exp, r=[sk[0]], w=[ppk], scale=0.125)
                po, pok = K.pbank[h // 4]
                off = (h % 4) * 65
                for i, kr in enumerate(keyrows):
                    K.mm(po[0:64, off:off + 65], p_[0:64, i * 64:(i + 1) * 64], Vr[0:64, kr, h, :], i == 0, i == nk - 1, r=[ppk, "Vr"], w=[pok])
            y_, yk = ybr[(r // 2) % 2], "ybr%d" % ((r // 2) % 2)
            qb = (r % 2) * 64
            for h in range(8):
                po, pok = K.pbank[h // 4]
                off = (h % 4) * 65
                K.S.op("vector", lambda e, po=po, off=off, h=h: e.reciprocal(out=rec[0:64, h:h + 1], in_=po[0:64, off + 64:off + 65]), r=[pok], w=["rec"])
                K.ts(y_[qb:qb + 64, h * 64:(h + 1) * 64], po[0:64, off:off + 64], rec[0:64, h:h + 1], None, ALU.mult, None, r=[pok, "rec"], w=[yk])
            if r % 2 == 0:
                continue
            i = r // 2
            ps, pk = K.proj_tm(Wz, "Wz", 0, 512, i * 128, 128)
            K.act(sz[i % 2][:, :], ps[:, 0:512], AF.Silu, r=[pk], w=["sz%d" % (i % 2)])
            K.tt(yzb[i % 2][:, :], y_[:, :], sz[i % 2][:, :], ALU.mult, r=[yk, "sz%d" % (i % 2)], w=["yzb%d" % (i % 2)])
            for c in range(4):
                pt = K.ptb[:, c * 128:c * 128 + 128]
                K.S.op("tensor", lambda e, pt=pt, i=i, c=c: e.transpose(pt, yzb[i % 2][:, c * 128:(c + 1) * 128], ident[:, :]),
                       r=["yzb%d" % (i % 2), "ident"], w=["Pptb"])
            K.copy(K.ev(), yzT[:, :, i * 128:(i + 1) * 128], K.ptb[:, 0:512].rearrange("p (c t) -> p c t", t=128), r=["Pptb"], w=["yzT"])
        c0_ = 0 if need_ctx else NC
        S.dma("sync", A["yzs"][1].rearrange("(c p) t -> p c t", p=128)[:, :, c0_:TT], yzT[:, :, c0_:TT], r=["yzT"], w=["yzs1"])
        S.barrier()


def gla_phase(K, l, b, need_ctx, A):
    S = K.S
    hT = K.hT
    ident = A["ident"]
    with contextlib.ExitStack() as ph:
        Wa = [K.sb(ph, "Wa", [128, 8, 512], BF16) for _ in range(2)]
        Wglr = K.sb(ph, "Wglr", [128, 8, 32], BF16)
        wgb = K.sb(ph, "wgb", [128, 512], BF16)
        gnorm = K.sb(ph, "gnorm", [128, 128], F32)
        tri = K.sb(ph, "tri", [128, 6, 128], F32)
        amask = K.sb(ph, "amask", [128, 2, 64], F32)
        qT = K.sb(ph, "qT", [128, 2, TT], BF16)
        kT = K.sb(ph, "kT", [128, 2, TT], BF16)
        ktm = K.sb(ph, "ktm", [128, 18, 256], BF16)
        vtm = K.sb(ph, "vtm", [128, 18, 512], BF16)
        glrT = K.sb(ph, "glrT", [128, TT], BF16)
        gtb = [K.sb(ph, "gtb", [128, 256], F32) for _ in range(3)]
        qtl = K.sb(ph, "qtl", [128, 2, TT], BF16)
        ktl = K.sb(ph, "ktl", [128, 2, TT], BF16)
        qin = K.sb(ph, "qin", [128, 2, TT], BF16)
        kout = K.sb(ph, "kout", [128, 18, 256], BF16)
        dec = K.sb(ph, "dec", [128, 2, 36], F32)
        of = K.sb(ph, "of", [128, 18, 512], F32)
        Sf = [K.sb(ph, "Sf", [128, 128], F32) for _ in range(4)]
        Sb = [K.sb(ph, "Sb", [128, 128], BF16) for _ in range(4)]
        aT = [K.sb(ph, "aT", [128, 64], BF16) for _ in range(4)]
        t1 = [K.sb(ph, "t1", [128, 256], F32) for _ in range(3)]
        t2 = [K.sb(ph, "t2", [128, 128], F32) for _ in range(4)]
        ss = K.sb(ph, "ss", [128, 4], F32)
        yg = [K.sb(ph, "yg", [128, 512], F32) for _ in range(2)]
        sz = [K.sb(ph, "sz", [128, 512], F32) for _ in range(2)]
        yzb = [K.sb(ph, "yzb", [128, 512], BF16) for _ in range(2)]
        yzt = [K.sb(ph, "yzt", [128, 4, 128], BF16) for _ in range(2)]
        K.load_w(Wa[0][:], "Wa0", l, C_GQK, C_GQK + 512)
        K.load_w(Wa[1][:], "Wa1", l, C_GV, C_GV + 512)
        K.load_w(Wglr[:], "Wglr", l, C_GLR, C_GLR + 32)
        S.dma("gpsimd", wgb[0:33, :], A["gla_wgblk"][l], w=["wgb"])
        S.dma("sync", gnorm[:], A["gla_normR"][l], w=["gnorm"])
        S.dma("sync", tri[:], A["c_tri"], w=["tri"])
        S.dma("sync", amask[0:64, :, :], A["c_amask"], w=["amask"])
        S.dma("sync", glrT[32:33, :], A["c_onesrow"], w=["glrT"])
        for ti, (t0, n) in enumerate(T512):
            for j in range(2):
                ps, pk = K.proj_fm(Wa[0], "Wa0", j * 128, 128, t0, n)
                K.copy("scalar", qT[:, j, t0:t0 + n], ps[:, 0:n], r=[pk], w=["qT"], scale=0.125)
                ps, pk = K.proj_fm(Wa[0], "Wa0", 256 + j * 128, 128, t0, n)
                K.copy("vector", kT[:, j, t0:t0 + n], ps[:, 0:n], r=[pk], w=["kT"])
            ps, pk = K.proj_fm(Wglr, "Wglr", 0, 32, t0, n)
            K.copy("vector", glrT[0:32, t0:t0 + n], ps[0:32, 0:n], r=[pk], w=["glrT"])
        for i in range(18):
            ps, pk = K.proj_tm(Wa[0], "Wa0", 256, 256, i * 128, 128)
            K.copy("scalar", ktm[:, i, :], ps[:, 0:256], r=[pk], w=["ktm"])
            ps, pk = K.proj_tm(Wa[1], "Wa1", 0, 512, i * 128, 128)
            K.copy("vector", vtm[:, i, :], ps[:, 0:512], r=[pk], w=["vtm"])
        K.load_w(Wa[0][:], "Wa0", l, C_GZ, C_GZ + 512)
        for dr in range(2):
            for i in range(18):
                ps, pk = K.ps()
                K.mm(ps[:, 0:256], glrT[0:33, i * 128:(i + 1) * 128], wgb[0:33, dr * 256:(dr + 1) * 256], True, True, r=["glrT", "wgb"], w=[pk])
                t_, tk = t1[i % 3], "t1_%d" % (i % 3)
                K.act(t_[:, :], ps[:, 0:256], AF.Exp, r=[pk], w=[tk], scale=-1.0)
                K.act(t_[:, :], t_[:, :], AF.Ln, r=[tk], w=[tk], bias=1.0)
                gtm_i, gk = gtb[i % 3], "gtb%d" % (i % 3)
                K.ts(gtm_i[:, :], t_[:, :], -1.0 / 16.0, None, ALU.mult, None, r=[tk], w=[gk])
                tl = slice(i * 128, (i + 1) * 128)
                for c in range(2):
                    gl = gtm_i[:, c * 128:(c + 1) * 128]
                    ps1, pk1 = K.ps()
                    K.mm(ps1[:, 0:128], gl, tri[:, 3 * dr + 2, :], True, True, r=[gk, "tri"], w=[pk1])
                    ps2, pk2 = K.ps()
                    K.mm(ps2[:, 0:128], gl, tri[:, 3 * dr + 0, :], True, True, r=[gk, "tri"], w=[pk2])
                    K.act(t2[0][:, :], ps1[:, 0:128], AF.Exp, r=[pk1], w=["t2_0"])
                    K.tt(qtl[:, c, tl], qT[:, c, tl], t2[0][:, :], ALU.mult, r=["qT", "t2_0"], w=["qtl"])
                    K.act(t2[1][:, :], ps1[:, 0:128], AF.Exp, r=[pk1], w=["t2_1"], scale=-1.0)
                    K.tt(ktl[:, c, tl], kT[:, c, tl], t2[1][:, :], ALU.mult, r=["kT", "t2_1"], w=["ktl"])
                    K.act(t2[2][:, :], ps2[:, 0:128], AF.Exp, r=[pk2], w=["t2_2"])
                    K.tt(qin[:, c, tl], qT[:, c, tl], t2[2][:, :], ALU.mult, r=["qT", "t2_2"], w=["qin"])
                    for half in range(2):
                        col = (63 if dr == 0 else 0) + 64 * half
                        K.copy("vector", dec[:, c, 2 * i + half:2 * i + half + 1], t2[2][:, col:col + 1], r=["t2_2"], w=["dec"])
                ps3, pk3 = K.ps()
                K.mm(ps3[:, 0:256], tri[:, 3 * dr + 1, :], gtm_i[:, :], True, True, r=[gk, "tri"], w=[pk3])
                K.act(t_[:, :], ps3[:, 0:256], AF.Exp, r=[pk3], w=[tk])
                K.tt(kout[:, i, :], ktm[:, i, :], t_[:, :], ALU.mult, r=["ktm", tk], w=["kout"])
            order = list(range(36)) if dr == 0 else [3, 2, 1, 0] + list(range(35, 3, -1))
            for hd in range(4):
                K.memset(Sf[hd][:], 0.0, w=["Sf%d" % hd])
                K.memset(Sb[hd][:], 0.0, w=["Sb%d" % hd])
            for ci in order:
                i, half = ci // 2, ci % 2
                tb, n0 = half * 64, ci * 64
                for hd in range(4):
                    c, pb = hd // 2, (hd % 2) * 64
                    vs = vtm[tb:tb + 64, i, hd * 128:(hd + 1) * 128]
                    if need_ctx or ci >= 4:
                        pa, pak = K.ps()
                        K.mm(pa[0:64, 0:64], ktl[pb:pb + 64, c, n0:n0 + 64], qtl[pb:pb + 64, c, n0:n0 + 64], True, True, r=["ktl", "qtl"], w=[pak])
                        K.tt(aT[hd][tb:tb + 64, :], pa[0:64, 0:64], amask[0:64, dr, :], ALU.mult, r=[pak, "amask"], w=["aT%d" % hd])
                        po, pok = K.ps()
                        K.mm(po[0:64, 0:128], qin[pb:pb + 64, c, n0:n0 + 64], Sb[hd][pb:pb + 64, :], True, False, r=["qin", "Sb%d" % hd], w=[pok])
                        K.mm(po[0:64, 0:128], aT[hd][tb:tb + 64, :], vs, False, True, r=["aT%d" % hd, "vtm"], w=[pok])
                        osl = of[tb:tb + 64, i, hd * 128:(hd + 1) * 128]
                        if dr == 0:
                            K.copy("scalar", osl, po[0:64, 0:128], r=[pok], w=["of"])
                        else:
                            K.copy("scalar", t2[3][tb:tb + 64, :], po[0:64, 0:128], r=[pok], w=["t2_3"])
                            K.tt(osl, osl, t2[3][tb:tb + 64, :], ALU.add, r=["t2_3", "of"], w=["of"], eng="gpsimd")
                    pst, pstk = K.ps()
                    K.mm(pst[0:64, 0:128], kout[tb:tb + 64, i, hd * 64:(hd + 1) * 64], vs, True, True, r=["kout", "vtm"], w=[pstk])
                    K.copy("scalar", t2[hd % 2][pb:pb + 64, :], pst[0:64, 0:128], r=[pstk], w=["t2_%d" % (hd % 2)])
                    K.stt(Sf[hd][pb:pb + 64, :], Sf[hd][pb:pb + 64, :], dec[pb:pb + 64, c, ci:ci + 1], t2[hd % 2][pb:pb + 64, :], ALU.mult, ALU.add,
                          r=["dec", "t2_%d" % (hd % 2), "Sf%d" % hd], w=["Sf%d" % hd])
                    K.copy("vector", Sb[hd][pb:pb + 64, :], Sf[hd][pb:pb + 64, :], r=["Sf%d" % hd], w=["Sb%d" % hd])
        for i in range(18):
            if i < 2 and not need_ctx:
                continue
            y_, yk = yg[i % 2], "yg%d" % (i % 2)
            K.tt(y_[:, :], of[:, i, :], of[:, i, :], ALU.mult, r=["of"], w=[yk])
            K.S.op("vector", lambda e, y_=y_: e.reduce_sum(out=ss[:, 0:4], in_=y_[:, :].rearrange("p (h d) -> p h d", d=128), axis=AX.X), r=[yk], w=["ss"])
            K.act(ss[:, 0:4], ss[:, 0:4], AF.Ln, r=["ss"], w=["ss"], scale=1.0 / 128, bias=K.epsc[:, 0:1])
            K.act(ss[:, 0:4], ss[:, 0:4], AF.Exp, r=["ss"], w=["ss"], scale=-0.5)
            for hd in range(4):
                K.stt(y_[:, hd * 128:(hd + 1) * 128], of[:, i, hd * 128:(hd + 1) * 128], ss[:, hd:hd + 1], gnorm[:, :], ALU.mult, ALU.mult,
                      r=["of", "ss", "gnorm"], w=[yk])
            ps, pk = K.proj_tm(Wa[0], "Wa0", 0, 512, i * 128, 128)
            K.act(sz[i % 2][:, :], ps[:, 0:512], AF.Silu, r=[pk], w=["sz%d" % (i % 2)])
            K.tt(yzb[i % 2][:, :], y_[:, :], sz[i % 2][:, :], ALU.mult, r=[yk, "sz%d" % (i % 2)], w=["yzb%d" % (i % 2)])
            for c in range(4):
                pt = K.ptb[:, c * 128:c * 128 + 128]
                K.S.op("tensor", lambda e, pt=pt, i=i, c=c: e.transpose(pt, yzb[i % 2][:, c * 128:(c + 1) * 128], ident[:, :]),
                       r=["yzb%d" % (i % 2), "ident"], w=["Pptb"])
            K.copy(K.ev(), yzt[i % 2][:, :, :], K.ptb[:, 0:512].rearrange("p (c t) -> p c t", t=128), r=["Pptb"], w=["yzt%d" % (i % 2)])
            S.dma("sync", A["yzs"][3].rearrange("(c p) t -> p c t", p=128)[:, :, i * 128:(i + 1) * 128], yzt[i % 2][:, :, :], r=["yzt%d" % (i % 2)], w=["yzs3"])
        S.barrier()


def hyena_filters(K, l, Lv, A):
    S = K.S
    ones = A["ones"]
    nt = Lv // 128
    sfx = "L" if Lv == L else "C"
    Hsp = A["Hsp" + sfx][l]
    dC = A["dC" + sfx].rearrange("(st p) k -> p st k", p=128)
    dS = A["dS" + sfx].rearrange("(st p) k -> p st k", p=128)
    PI = math.pi
    with contextlib.ExitStack() as ph:
        zp = K.sb(ph, "zp", [128, Lv], F32)
        w1 = K.sb(ph, "w1", [128, 64], F32)
        w2 = K.sb(ph, "w2", [128, 64], F32)
        w3 = K.sb(ph, "w3", [128, 2048], F32)
        bb = K.sb(ph, "bb", [128, 2], F32)
        fr = K.sb(ph, "fr", [128, 2], F32)
        a1 = K.sb(ph, "a1", [128, Lv], F32)
        a2 = K.sb(ph, "a2", [128, Lv], F32)
        wr = K.sb(ph, "wr", [128, 512], F32)
        delta = K.sb(ph, "delta", [128, 512], F32)
        ntn = K.sb(ph, "ntn", [128, 16], F32)
        fw = K.sb(ph, "fw", [128, nt, 2, 512], F32)
        hsb = K.sb(ph, "hsb", [128, nt, 512], BF16)
        hdb = K.sb(ph, "hdb", [128, nt, 512], BF16)
        dct = [K.sb(ph, "dct", [128, nt, 128], BF16) for _ in range(2)]
        dst = [K.sb(ph, "dst", [128, nt, 128], BF16) for _ in range(2)]
        win = K.sb(ph, "win", [128, 512], F32)
        abb = [K.sb(ph, "abb", [128, 512], BF16) for _ in range(2)]
        inv = K.sb(ph, "inv", [128, 512], F32)
        tq = [K.sb(ph, "tq", [128, 512], F32) for _ in range(2)]
        stg = [K.sb(ph, "stg", [128, 512], F32) for _ in range(2)]
        S.dma("sync", zp[0:33, :], A["zpos" + sfx], w=["zp"])
        S.dma("sync", w1[0:33, :], A["hy_pe_w1"][l], w=["w1"])
        S.dma("sync", w2[0:64, :], A["hy_pe_w2"][l], w=["w2"])
        S.dma("sync", w3[0:64, :], A["hy_pe_w3"][l], w=["w3"])
        S.dma("sync", bb[0:64, :], A["hy_pe_bT"][l], w=["bb"])
        S.dma("sync", fr[0:64, :], A["hy_pe_freqT"][l], w=["fr"])
        S.dma("sync", delta[:], A["c_delta"], w=["delta"])
        S.dma("sync", ntn[:, 0:nt], A["ntn" + sfx], w=["ntn"])
        tiles = [(t0, min(512, Lv - t0)) for t0 in range(0, Lv, 512)]
        for stage, (wm, kk, src, dstt, nk) in enumerate(((w1, "w1", zp, a1, 33), (w2, "w2", a1, a2, 64))):
            sk, dk = ("zp", "a1") if stage == 0 else ("a1", "a2")
            for (t0, n) in tiles:
                ps, pk = K.ps()
                K.mm(ps[0:64, 0:n], wm[0:nk, 0:64], src[0:nk, t0:t0 + n], True, True, r=[kk, sk], w=[pk])
                y = dstt[0:64, t0:t0 + n]
                K.ts(y, ps[0:64, 0:n], bb[0:64, stage:stage + 1], fr[0:64, stage:stage + 1], ALU.add, ALU.mult, r=[pk, "bb", "fr"], w=[dk])
                for it in range(2):
                    K.ts(wr[0:64, 0:n], y, -PI, 2 * PI, ALU.is_lt, ALU.mult, r=[dk], w=["wr"])
                    K.tt(y, y, wr[0:64, 0:n], ALU.add, r=[dk, "wr"], w=[dk])
                    K.ts(wr[0:64, 0:n], y, PI, 2 * PI, ALU.is_gt, ALU.mult, r=[dk], w=["wr"])
                    K.tt(y, y, wr[0:64, 0:n], ALU.subtract, r=[dk, "wr"], w=[dk])
                K.act(y, y, AF.Sin, r=[dk], w=[dk])
        pss, pssk = K.pbank[2]
        for n_ in range(2):
            cnt = 0
            for i in range(nt):
                K.act(win[:, :], delta[:, :], AF.Exp, r=["delta", "ntn"], w=["win"], scale=ntn[:, i:i + 1])
                for dn in range(2):
                    ps, pk = K.ps()
                    c0 = (n_ * 2 + dn) * 512
                    K.mm(ps[:, 0:512], a2[0:64, i * 128:(i + 1) * 128], w3[0:64, c0:c0 + 512], True, True, r=["a2", "w3"], w=[pk])
                    K.tt(fw[:, i, dn, :], ps[:, 0:512], win[:, :], ALU.mult, r=[pk, "win"], w=["fw"])
                    ab, abk = abb[cnt % 2], "abb%d" % (cnt % 2)
                    K.act(ab[:, :], fw[:, i, dn, :], AF.Abs, r=["fw"], w=[abk])
                    K.mm(pss[:, 0:512], ones[:, :], ab[:, :], cnt == 0, cnt == 2 * nt - 1, r=["ones", abk], w=[pssk])
                    cnt += 1
            K.ts(inv[:, :], pss[:, 0:512], EPS, None, ALU.add, None, r=[pssk], w=["inv"])
            K.S.op("vector", lambda e: e.reciprocal(out=inv[:, :], in_=inv[:, :]), r=["inv"], w=["inv"])
            K.memset(fw[0:1, 0, 1, :], 0.0, w=["fw"])
            for i in range(nt):
                K.tt(tq[0][:, :], fw[:, i, 0, :], fw[:, i, 1, :], ALU.add, r=["fw"], w=["tq0"])
                K.tt(hsb[:, i, :], tq[0][:, :], inv[:, :], ALU.mult, r=["tq0", "inv"], w=["hsb"])
                K.tt(tq[1][:, :], fw[:, i, 0, :], fw[:, i, 1, :], ALU.subtract, r=["fw"], w=["tq1"], eng="gpsimd")
                K.tt(hdb[:, i, :], tq[1][:, :], inv[:, :], ALU.mult, r=["tq1", "inv"], w=["hdb"], eng="gpsimd")
            for kt in range(nt):
                dc, dck = dct[kt % 2], "dct%d" % (kt % 2)
                ds_, dsk = dst[kt % 2], "dst%d" % (kt % 2)
                S.dma("sync", dc[:, :, :], dC[:, :, kt * 128:(kt + 1) * 128], w=[dck])
                S.dma("sync", ds_[:, :, :], dS[:, :, kt * 128:(kt + 1) * 128], w=[dsk])
                for which, (dd, ddk, hb, hbk) in enumerate(((dc, dck, hsb, "hsb"), (ds_, dsk, hdb, "hdb"))):
                    ps, pk = K.ps()
                    for st_ in range(nt):
                        K.mm(ps[:, 0:512], dd[:, st_, :], hb[:, st_, :], st_ == 0, st_ == nt - 1, r=[ddk, hbk], w=[pk])
                    sg_, sgk = stg[which], "stg%d" % which
                    K.copy(K.ev(), sg_[:, :], ps[:, 0:512], r=[pk], w=[sgk])
                    S.dma("sync", Hsp[n_, which, kt * 128:(kt + 1) * 128, :], sg_[:, :], r=[sgk], w=["Hsp%s%d" % (sfx, l)])
        S.barrier()


def hyena_phase(K, l, b, need_ctx, A, Lv, off):
    S = K.S
    hT = K.hT
    ident = A["ident"]
    nt = Lv // 128
    sfx = "L" if Lv == L else "C"
    Hsp = A["Hsp" + sfx][l]
    dC = A["dC" + sfx].rearrange("(st p) k -> p st k", p=128)
    dS = A["dS" + sfx].rearrange("(st p) k -> p st k", p=128)
    dCt = A["dCt" + sfx]
    dSt = A["dSt" + sfx]
    tiles = [(t0, min(512, Lv - t0)) for t0 in range(0, Lv, 512)]
    with contextlib.ExitStack() as ph:
        Wh = [K.sb(ph, "Wh", [128, 8, 128], BF16) for _ in range(2)]
        cw = K.sb(ph, "cw", [128, 12, 3], F32)
        cb = K.sb(ph, "cb", [128, 12], F32)
        skp = K.sb(ph, "skp", [128, 2, 4], F32)
        ub = K.sb(ph, "ub", [128, Lv + 2], F32)
        ycv = K.sb(ph, "ycv", [128, Lv], F32)
        xg = [K.sb(ph, "xg", [128, 4, Lv], BF16) for _ in range(2)]
        zT = K.sb(ph, "zT", [128, 4, Lv], F32)
        zt = K.sb(ph, "zt", [128, nt, 512], BF16)
        Yre = K.sb(ph, "Yre", [128, nt, 512], BF16)
        Ys = K.sb(ph, "Ys", [128, nt, 512], BF16)
        dct = [K.sb(ph, "dct", [128, nt, 128], BF16) for _ in range(1)]
        dst = [K.sb(ph, "dst", [128, nt, 128], BF16) for _ in range(1)]
        Hre = [K.sb(ph, "Hre", [128, 512], F32) for _ in range(1)]
        Hs = [K.sb(ph, "Hs", [128, 512], F32) for _ in range(1)]
        cti = [K.sb(ph, "cti", [128, 512], BF16) for _ in range(2)]
        sti = [K.sb(ph, "sti", [128, 512], BF16) for _ in range(2)]
        tq = [K.sb(ph, "tq", [128, 512], F32) for _ in range(3)]
        yzt = [K.sb(ph, "yzt", [128, 512], BF16) for _ in range(2)]
        identf = K.sb(ph, "identf", [128, 128], F32)
        S.dma("sync", cw[:], A["hy_conv_wT"][l], w=["cw"])
        S.dma("sync", cb[:], A["hy_conv_bT"][l], w=["cb"])
        S.dma("sync", skp[:], A["hy_skipT"][l], w=["skp"])
        S.dma("sync", identf[:], A["c_identf"], w=["identf"])
        K.memset(ub[:, :], 0.0, w=["ub"])
        for cch in range(12):
            wt, wk = Wh[cch % 2], "Wh%d" % (cch % 2)
            K.load_w(wt[:], wk, l, C_HP + cch * 128, C_HP + (cch + 1) * 128)
            for (t0, n) in tiles:
                ps, pk = K.proj_fm(wt, wk, 0, 128, off + t0, n)
                K.copy("scalar", ub[:, 1 + t0:1 + t0 + n], ps[:, 0:n], r=[pk], w=["ub"])
            if cch < 8:
                yc, yk = ycv[:, :], "ycv"
                dstc, dk = xg[cch // 4][:, cch % 4, :], "xg%d" % (cch // 4)
            else:
                yc, yk = zT[:, cch - 8, :], "zT"
                dstc, dk = yc, yk
            K.ts(yc, ub[:, 1:Lv + 1], cw[:, cch, 1:2], cb[:, cch:cch + 1], ALU.mult, ALU.add, r=["ub", "cw", "cb"], w=[yk])
            K.stt(yc, ub[:, 0:Lv], cw[:, cch, 0:1], yc, ALU.mult, ALU.add, r=["ub", "cw", yk], w=[yk])
            K.stt(dstc, ub[:, 2:Lv + 2], cw[:, cch, 2:3], yc, ALU.mult, ALU.add, r=["ub", "cw", yk], w=[dk])
        for n_ in range(2):
            for i in range(nt):
                ps, pk = K.ps()
                for c in range(4):
                    K.S.op("tensor", lambda e, ps=ps, i=i, c=c: e.transpose(ps[:, c * 128:(c + 1) * 128], zT[:, c, i * 128:(i + 1) * 128], identf[:, :]),
                           r=["zT", "identf"], w=[pk])
                K.copy(K.ev(), zt[:, i, :], ps[:, 0:512], r=[pk], w=["zt"])
            for kt in range(nt):
                dc, dck = dct[0], "dct0"
                ds_, dsk = dst[0], "dst0"
                hr, hrk = Hre[0], "Hre0"
                hs, hsk = Hs[0], "Hs0"
                S.dma("sync", dc[:, :, :], dC[:, :, kt * 128:(kt + 1) * 128], w=[dck])
                S.dma("sync", ds_[:, :, :], dS[:, :, kt * 128:(kt + 1) * 128], w=[dsk])
                S.dma("sync", hr[:, :], Hsp[n_, 0, kt * 128:(kt + 1) * 128, :], r=["Hsp%s%d" % (sfx, l)], w=[hrk])
                S.dma("sync", hs[:, :], Hsp[n_, 1, kt * 128:(kt + 1) * 128, :], r=["Hsp%s%d" % (sfx, l)], w=[hsk])
                psA, pkA = K.ps()
                for st_ in range(nt):
                    K.mm(psA[:, 0:512], dc[:, st_, :], zt[:, st_, :], st_ == 0, st_ == nt - 1, r=[dck, "zt"], w=[pkA])
                psB, pkB = K.ps()
                for st_ in range(nt):
                    K.mm(psB[:, 0:512], ds_[:, st_, :], zt[:, st_, :], st_ == 0, st_ == nt - 1, r=[dsk, "zt"], w=[pkB])
                K.copy("scalar", tq[2][:, :], psB[:, 0:512], r=[pkB], w=["tq2"])
                K.tt(tq[0][:, :], psA[:, 0:512], hr[:, :], ALU.mult, r=[pkA, hrk], w=["tq0"])
                K.tt(tq[1][:, :], tq[2][:, :], hs[:, :], ALU.mult, r=["tq2", hsk], w=["tq1"], eng="gpsimd")
                K.tt(Yre[:, kt, :], tq[0][:, :], tq[1][:, :], ALU.subtract, r=["tq0", "tq1"], w=["Yre"])
                K.tt(tq[0][:, :], psA[:, 0:512], hs[:, :], ALU.mult, r=[pkA, hsk], w=["tq0"])
                K.tt(tq[1][:, :], tq[2][:, :], hr[:, :], ALU.mult, r=["tq2", hrk], w=["tq1"], eng="gpsimd")
                K.tt(Ys[:, kt, :], tq[0][:, :], tq[1][:, :], ALU.add, r=["tq0", "tq1"], w=["Ys"])
            for (t0, n) in tiles:
                accs = [(K.pbank[0][0], K.pbank[0][1]), (K.pbank[1][0], K.pbank[1][1]), (K.pbank[2][0], K.pbank[2][1]),
                        (K.q2[0][:, 0:512], "Pq0a")]
                for kt in range(nt):
                    ct, ctk = cti[kt % 2], "cti%d" % (kt % 2)
                    st2, stk = sti[kt % 2], "sti%d" % (kt % 2)
                    S.dma("sync", ct[:, 0:n], dCt[kt * 128:(kt + 1) * 128, t0:t0 + n], w=[ctk])
                    S.dma("sync", st2[:, 0:n], dSt[kt * 128:(kt + 1) * 128, t0:t0 + n], w=[stk])
                    for c in range(4):
                        pa, pak = accs[c]
                        K.mm(pa[:, 0:n], Yre[:, kt, c * 128:(c + 1) * 128], ct[:, 0:n], kt == 0, False, r=["Yre", ctk], w=[pak])
                        K.mm(pa[:, 0:n], Ys[:, kt, c * 128:(c + 1) * 128], st2[:, 0:n], False, kt == nt - 1, r=["Ys", stk], w=[pak], inc=(c == 3 or kt == nt - 1))
                for c in range(4):
                    pa, pak = accs[c]
                    zs = zT[:, c, t0:t0 + n]
                    K.ts(tq[0][:, 0:n], zs, skp[:, n_, c:c + 1], None, ALU.mult, None, r=["zT", "skp"], w=["tq0"])
                    K.stt(tq[1][:, 0:n], pa[:, 0:n], 1.0 / Lv, tq[0][:, 0:n], ALU.mult, ALU.add, r=[pak, "tq0"], w=["tq1"])
                    K.tt(zs, tq[1][:, 0:n], xg[n_][:, c, t0:t0 + n], ALU.mult, r=["tq1", "xg%d" % n_], w=["zT"])
        yzd = A["yzs"][2].rearrange("(c p) t -> p c t", p=128)
        it = 0
        for c in range(4):
            wt, wk = Wh[c % 2], "Wh%d" % (c % 2)
            K.load_w(wt[:], wk, l, C_HZ + c * 128, C_HZ + (c + 1) * 128)
            for (t0, n) in tiles:
                ps, pk = K.proj_fm(wt, wk, 0, 128, off + t0, n)
                K.act(tq[0][:, 0:n], ps[:, 0:n], AF.Silu, r=[pk], w=["tq0"])
                y_, yk = yzt[it % 2], "yzt%d" % (it % 2)
                it += 1
                K.tt(y_[:, 0:n], zT[:, c, t0:t0 + n], tq[0][:, 0:n], ALU.mult, r=["zT", "tq0"], w=[yk])
                S.dma("sync", yzd[:, c, off + t0:off + t0 + n], y_[:, 0:n], r=[yk], w=["yzs2"])
        S.barrier()


def tq_big(K, ph, Lv):
    if not hasattr(K, "_tqb") or K._tqb[0] is not ph:
        K._tqb = (ph, K.sb(ph, "ycv", [128, Lv], F32))
    return K._tqb[1][:, :], "ycv"


def merge_phase(K, l, b, need_ctx, A):
    S = K.S
    hT = K.hT
    ones = A["ones"]
    with contextlib.ExitStack() as ph:
        yz = K.sb(ph, "yz", [128, 4, 4, TT], BF16)
        mg = K.sb(ph, "mg", [128, 8, TT], BF16)
        Wg = [K.sb(ph, "Wg", [128, 8, 4, 128], BF16) for _ in range(1)]
        Wb = [K.sb(ph, "Wb", [128, 4, 4, 128], BF16) for _ in range(1)]
        Wo = K.sb(ph, "Wo", [128, 8, D], BF16)
        sg = [K.sb(ph, "sg", [128, 512], F32) for _ in range(2)]
        acc = [K.sb(ph, "acc", [128, 512], F32) for _ in range(2)]
        of = K.sb(ph, "of", [128, 8, 512], F32)
        osq = [K.sb(ph, "osq", [128, 512], BF16) for _ in range(2)]
        xb2 = [K.sb(ph, "xb2", [128, 512], F32) for _ in range(2)]
        rs = K.sb(ph, "rs", [128, 512], F32)
        for i in range(4):
            c0_ = 0 if need_ctx else NC
            S.dma("sync", yz[:, i, :, c0_:TT], A["yzs"][i].rearrange("(c p) t -> p c t", p=128)[:, :, c0_:TT], r=["yzs%d" % i], w=["yz"])
        S.dma("gpsimd", Wo[:], A["w_out"][l].rearrange("(k p) c -> p k c", p=128), w=["Wo"])
        tiles = [t for t in T512 if need_ctx or t[0] >= NC]
        for fc in range(8):
            wg, wgk = Wg[0], "Wg0"
            wb_, wbk = Wb[0], "Wb0"
            for i in range(4):
                K.load_w(wg[:, :, i, :], wgk, l, C_MG + i * 1024 + fc * 128, C_MG + i * 1024 + (fc + 1) * 128)
                S.dma("gpsimd", wb_[:, i, :, :], A["w_branch"][l][i].rearrange("(k p) c -> p k c", p=128)[:, :, fc * 128:(fc + 1) * 128], w=[wbk])
            for ti, (t0, n) in enumerate(tiles):
                a_, ak = acc[ti % 2], "acc%d" % (ti % 2)
                for i in range(4):
                    pg, pgk = K.ps()
                    for k in range(8):
                        K.mm(pg[:, 0:n], wg[:, k, i, :], hT[:, k, t0:t0 + n], k == 0, k == 7, r=[wgk, "hT"], w=[pgk])
                    pb, pbk = K.ps()
                    for k in range(4):
                        K.mm(pb[:, 0:n], wb_[:, i, k, :], yz[:, i, k, t0:t0 + n], k == 0, k == 3, r=[wbk, "yz"], w=[pbk])
                    s_, sk = sg[i % 2], "sg%d" % (i % 2)
                    K.act(s_[:, 0:n], pg[:, 0:n], AF.Sigmoid, r=[pgk], w=[sk])
                    if i == 0:
                        K.tt(a_[:, 0:n], s_[:, 0:n], pb[:, 0:n], ALU.mult, r=[sk, pbk], w=[ak])
                    else:
                        K.tt(s_[:, 0:n], s_[:, 0:n], pb[:, 0:n], ALU.mult, r=[sk, pbk], w=[sk])
                        if i < 3:
                            K.tt(a_[:, 0:n], a_[:, 0:n], s_[:, 0:n], ALU.add, r=[sk, ak], w=[ak], eng="gpsimd")
                        else:
                            K.tt(mg[:, fc, t0:t0 + n], a_[:, 0:n], s_[:, 0:n], ALU.add, r=[sk, ak], w=["mg"], eng="gpsimd")
        for ti, (t0, n) in enumerate(tiles):
            col = 2 if t0 < NC else b
            pss, pssk = K.pbank[2]
            for fo in range(8):
                ps, pk = K.ps()
                for k in range(8):
                    K.mm(ps[:, 0:n], Wo[:, k, fo * 128:(fo + 1) * 128], mg[:, k, t0:t0 + n], k == 0, k == 7, r=["Wo", "mg"], w=[pk])
                K.copy("scalar", of[:, fo, 0:n], ps[:, 0:n], r=[pk], w=["of"])
                sq_, sqk = osq[fo % 2], "osq%d" % (fo % 2)
                K.tt(sq_[:, 0:n], of[:, fo, 0:n], of[:, fo, 0:n], ALU.mult, r=["of"], w=[sqk])
                K.mm(pss[:, 0:n], ones[:, :], sq_[:, 0:n], fo == 0, fo == 7, r=["ones", sqk], w=[pssk])
            K.rstd(pss, pssk, rs, "rsm", n, D)
            for k in range(8):
                x_, xk = xb2[k % 2], "xb2_%d" % (k % 2)
                S.dma("sync", x_[:, 0:n], A["xsrc"][:, k, t0:t0 + n], w=[xk])
                K.stt(of[:, k, 0:n], of[:, k, 0:n], A["modG"][l][:, k, col:col + 1], rs[:, 0:n], ALU.mult, ALU.mult,
                      r=["of", "modG%d" % l, "rsm"], w=["of"])
                K.tt(x_[:, 0:n], x_[:, 0:n], of[:, k, 0:n], ALU.add, r=["of", xk], w=[xk], eng="gpsimd")
                S.dma("sync", A["xdst"][:, k, t0 + A["doff"]:t0 + A["doff"] + n], x_[:, 0:n], r=[xk], w=["outdone"])
        S.barrier()


def _consts():
    c = {}
    c["c_ident"] = np.eye(128, dtype=np.float32).astype(ml_dtypes.bfloat16)
    c["c_ones"] = np.ones((128, 128), dtype=np.float32).astype(ml_dtypes.bfloat16)
    t = np.arange(L)
    row = (t // 64).astype(np.float32)
    colp = (t % 64).astype(np.float32)
    inv = np.power(np.float32(10000.0), -np.arange(0, 16, 2, dtype=np.float32) / np.float32(16)).astype(np.float32)
    ar, ac = row[:, None] * inv, colp[:, None] * inv
    ang = np.concatenate([ar, ar, ac, ac], -1)
    cos = np.cos(ang).astype(np.float32)
    sin = np.sin(ang).astype(np.float32)
    sgn = np.concatenate([-np.ones(8), np.ones(8), -np.ones(8), np.ones(8)]).astype(np.float32)
    rc = np.zeros((128, TT), np.float32)
    rs = np.zeros((128, TT), np.float32)
    rc[64:96, :NC] = 1.0
    rc[64:96, NC:] = cos.T
    rs[64:96, NC:] = (sin * sgn[None, :]).T
    c["c_ropec"], c["c_ropes"] = rc, rs
    j = np.arange(64)
    c0 = np.clip(j - 8, 0, 48)
    colmask = (j[None, :] >= c0[:, None]) & (j[None, :] < c0[:, None] + 16)
    m = np.where(colmask.T, 0.0, -30000.0 * 8).astype(np.float32)
    c["c_namask"] = np.ascontiguousarray(np.broadcast_to(m[:, None, :], (64, 15, 64)).reshape(64, 15 * 64))
    for sfx, n_ in (("L", L), ("C", NC)):
        tt_ = np.arange(n_, dtype=np.float64)
        tn = tt_ / max(n_ - 1, 1)
        frq = np.linspace(1e-4, 15, 16)
        ang = (2.0 * math.pi / n_) * tt_[:, None] * frq[None, :]
        z = np.concatenate([tn[:, None], np.cos(ang), -np.sin(ang)], -1)
        c["zpos" + sfx] = np.ascontiguousarray(z.T.astype(np.float32))
        c["ntn" + sfx] = np.ascontiguousarray((-tn).reshape(n_ // 128, 128).T.astype(np.float32))
        N = 2 * n_
        th = math.pi * (2 * np.arange(n_, dtype=np.float64)[None, :] + 1) * tt_[:, None] / N
        C_, S_ = np.cos(th), np.sin(th)
        bf = lambda a: np.ascontiguousarray(a.astype(np.float32)).astype(ml_dtypes.bfloat16)
        c["dC" + sfx], c["dS" + sfx], c["dCt" + sfx], c["dSt" + sfx] = bf(C_), bf(S_), bf(C_.T), bf(S_.T)
    ntl = c["ntnL"]
    c["ntnL"] = np.ascontiguousarray(ntl)
    mind, maxd = math.log(1e-2) / 1.5, math.log(1e-2) / 0.3
    dl = np.abs(np.linspace(mind, maxd, 512)).astype(np.float32)
    c["c_delta"] = np.ascontiguousarray(np.broadcast_to(dl[None, :], (128, 512)))
    c["c_identf"] = np.eye(128, dtype=np.float32)
    st = np.arange(128)
    same = (st[:, None] // 64) == (st[None, :] // 64)
    s_, t_ = st[:, None], st[None, :]
    triI_f = same & (s_ <= t_)
    triS_f = same & (s_ > t_)
    mid_f = same & ((s_ % 64) <= 31)
    triI_b = same & (s_ >= t_)
    triS_b = same & (s_ < t_)
    mid_b = same & ((s_ % 64) >= 32)
    tri = np.stack([triI_f, triS_f, triI_f.astype(np.float32) - mid_f, triI_b, triS_b, triI_b.astype(np.float32) - mid_b], 1)
    c["c_tri"] = np.ascontiguousarray(tri.astype(np.float32))
    a = np.arange(64)
    c["c_amask"] = np.ascontiguousarray(np.stack([a[:, None] <= a[None, :], a[:, None] >= a[None, :]], 1).astype(np.float32))
    c["c_onesrow"] = np.ones((1, TT), np.float32).astype(ml_dtypes.bfloat16)
    return c


def _prep(inputs):
    f = lambda a: np.ascontiguousarray(np.asarray(a, dtype=np.float32))
    p = {}
    p["ada_w"] = f(inputs["ada_w"])
    p["ada_bT"] = f(inputs["ada_b"].reshape(2, 24, 128).transpose(0, 2, 1))
    p["pre_gT"] = f(inputs["pre_g"].reshape(2, 8, 128).transpose(0, 2, 1))
    p["post_gT"] = f(inputs["post_g"].reshape(2, 8, 128).transpose(0, 2, 1))
    p["w_in"] = f(inputs["w_in"])
    p["w_kr_sw"] = f(inputs["w_in"][:, :, C_KR:C_KR + 32][:, :, ROT_PERM])
    p["mla_q_normT"] = f(inputs["mla_q_norm"].reshape(2, 2, 128).transpose(0, 2, 1))
    p["mla_kv_normT"] = f(inputs["mla_kv_norm"].reshape(2, 1, 128).transpose(0, 2, 1))
    p["mla_w_uq"] = f(inputs["mla_w_uq"])
    uq = np.asarray(inputs["mla_w_uq"]).reshape(2, 256, 8, 96).copy()
    uq[:, :, :, 64:96] = uq[:, :, :, 64:96][:, :, :, ROT_PERM]
    p["mla_w_uq_sw"] = f(uq.reshape(2, 256, 768))
    p["mla_w_ukv"] = f(inputs["mla_w_ukv"])
    j = np.arange(64)
    dcol = np.clip(j[:, None] - j[None, :], -15, 15) + 15
    rpb = np.asarray(inputs["na_rpb"])
    nb = rpb[:, :, :, dcol]
    p["na_bias"] = f(nb.transpose(0, 3, 1, 2, 4).reshape(2, 64, 8, 15 * 64))
    p["hy_conv_wT"] = f(np.asarray(inputs["hy_conv_w"]).reshape(2, 3, 12, 128).transpose(0, 3, 2, 1))
    p["hy_conv_bT"] = f(np.asarray(inputs["hy_conv_b"]).reshape(2, 12, 128).transpose(0, 2, 1))
    p["hy_skipT"] = f(np.asarray(inputs["hy_skip"]).reshape(2, 2, 4, 128).transpose(0, 3, 1, 2))
    p["hy_pe_w1"] = f(inputs["hy_pe_w1"])
    p["hy_pe_w2"] = f(inputs["hy_pe_w2"])
    p["hy_pe_w3"] = f(inputs["hy_pe_w3"])
    p["hy_pe_bT"] = f(np.stack([np.asarray(inputs["hy_pe_b1"]), np.asarray(inputs["hy_pe_b2"])], -1))
    p["hy_pe_freqT"] = f(np.asarray(inputs["hy_pe_freq"]).transpose(0, 2, 1))
    wg2 = np.asarray(inputs["gla_wg2"], dtype=np.float32)
    bg = np.asarray(inputs["gla_bg"], dtype=np.float32)
    blk = np.zeros((2, 33, 512), np.float32)
    blk[:, 0:16, 0:256] = wg2[:, 0]
    blk[:, 16:32, 256:512] = wg2[:, 1]
    blk[:, 32, 0:256] = bg[:, 0]
    blk[:, 32, 256:512] = bg[:, 1]
    p["gla_wgblk"] = blk
    p["gla_normR"] = f(np.broadcast_to(np.asarray(inputs["gla_norm"])[:, None, :], (2, 128, 128)))
    p["w_branch"] = f(inputs["w_branch"])
    p["w_out"] = f(inputs["w_out"])
    return p


def _core_inputs(inputs, core, shared):
    x = np.asarray(inputs["x"], dtype=np.float32)
    ctx = np.asarray(inputs["ctx"], dtype=np.float32)
    c = np.asarray(inputs["c"], dtype=np.float32)
    cc = np.asarray(inputs["c_ctx"], dtype=np.float32)
    b0 = 2 * core
    m = dict(shared)
    seq = np.concatenate([ctx[b0:b0 + 2], x[b0:b0 + 2]], axis=1)
    m["xT"] = np.ascontiguousarray(seq.transpose(0, 2, 1))
    cm = np.stack([c[b0], c[b0 + 1], cc], axis=-1)
    m["cT"] = np.ascontiguousarray(cm.reshape(8, 128, 3).transpose(1, 0, 2))
    return m


def kernel(**inputs):
    nc, K = build()
    shared = dict(_consts())
    shared.update(_prep(inputs))
    shared = {k: v for k, v in shared.items() if k in K.din}
    in_maps = [_core_inputs(inputs, c, shared) for c in range(8)]
    res = run_bass_kernel_spmd(nc, in_maps, core_ids=list(range(8)))
    outs = [np.asarray(r["outT"]).transpose(0, 2, 1) for r in res.results]
    return np.ascontiguousarray(np.concatenate(outs, axis=0).astype(np.float32))
```
